# Optimizing a Trainium2 kernel written in Bass

```python
import math
import jax, jax.numpy as jnp
from jax import lax
import numpy as np

D_MODEL = 1024
BATCH = 2
SEQ = 8192
DEPTH = 1

D_FF = 2816
D_CONV = D_MODEL
CONV_WIDTH = 31
HEAD_DIM = 64
N_HEADS = D_MODEL // HEAD_DIM
N_KV_HEADS = 4
GROUP = N_HEADS // N_KV_HEADS
WINDOW = 128
ROPE_THETA = 10000.0
EPS = 1e-6
LN_EPS = 1e-5
NEG_INF = -1e30

SPLITS = (D_CONV, D_CONV, N_HEADS * HEAD_DIM, N_KV_HEADS * HEAD_DIM, N_KV_HEADS * HEAD_DIM, D_MODEL, D_MODEL)
D_IN = sum(SPLITS)

kernel_name = "hybrid_macaron_conv_swa_gated_block"


def rmsnorm(x, g):
    xf = x.astype(jnp.float32)
    y = xf * lax.rsqrt(jnp.mean(xf * xf, axis=-1, keepdims=True) + EPS)
    return (y * g.astype(jnp.float32)).astype(x.dtype)


def layernorm(x, g, b):
    xf = x.astype(jnp.float32)
    mu = jnp.mean(xf, axis=-1, keepdims=True)
    var = jnp.mean(jnp.square(xf - mu), axis=-1, keepdims=True)
    y = (xf - mu) * lax.rsqrt(var + LN_EPS)
    return (y * g.astype(jnp.float32) + b.astype(jnp.float32)).astype(x.dtype)


def swiglu(x, w_gate, w_up, w_down):
    return (jax.nn.silu(x @ w_gate) * (x @ w_up)) @ w_down


def rope(x, positions):
    half = HEAD_DIM // 2
    inv_freq = ROPE_THETA ** (-jnp.arange(half, dtype=jnp.float32) / half)
    ang = positions.astype(jnp.float32)[..., None] * inv_freq
    cos = jnp.cos(ang)[:, :, None, :]
    sin = jnp.sin(ang)[:, :, None, :]
    xf = x.astype(jnp.float32)
    x1, x2 = xf[..., :half], xf[..., half:]
    out = jnp.concatenate([x1 * cos - x2 * sin, x2 * cos + x1 * sin], axis=-1)
    return out.astype(x.dtype)


def causal_depthwise_conv(u, w, b):
    out = lax.conv_general_dilated(
        u, w[:, None, :].astype(u.dtype), window_strides=(1,),
        padding=((CONV_WIDTH - 1, 0),),
        dimension_numbers=("NWC", "WIO", "NWC"),
        feature_group_count=u.shape[-1])
    return out + b.astype(u.dtype)


def conformer_conv_branch(glu_a, glu_b, dw_w, dw_b, ln_g, ln_b, w_proj):
    u = glu_a * jax.nn.sigmoid(glu_b)
    u = causal_depthwise_conv(u, dw_w, dw_b)
    u = jax.nn.silu(layernorm(u, ln_g, ln_b))
    return u @ w_proj


def band(t):
    B, S = t.shape[:2]
    nb = S // WINDOW
    tb = t.reshape(B, nb, WINDOW, N_KV_HEADS, HEAD_DIM)
    prev = jnp.pad(tb[:, :-1], ((0, 0), (1, 0), (0, 0), (0, 0), (0, 0)))
    return jnp.concatenate([prev, tb], axis=2)


def sliding_window_gqa_sinks(q, k, v, sinks):
    B, S = q.shape[:2]
    nb = S // WINDOW
    qb = q.reshape(B, nb, WINDOW, N_KV_HEADS, GROUP, HEAD_DIM)
    kb, vb = band(k), band(v)
    scores = jnp.einsum("bnqkgd,bnskd->bnkgqs", qb, kb).astype(jnp.float32) * (HEAD_DIM ** -0.5)
    qi = jnp.arange(WINDOW)[:, None]
    sj = jnp.arange(2 * WINDOW)[None, :] - WINDOW
    rel = qi - sj
    allowed = (rel >= 0) & (rel < WINDOW)
    blk = jnp.arange(nb)[:, None, None]
    allowed = allowed[None] & ((blk > 0) | (sj[None] >= 0))
    scores = jnp.where(allowed[None, :, None, None], scores, NEG_INF)
    sink = sinks.astype(jnp.float32).reshape(N_KV_HEADS, GROUP)[None, None, :, :, None, None]
    m = jnp.maximum(jnp.max(scores, axis=-1, keepdims=True), sink)
    p = jnp.exp(scores - m)
    probs = p / (jnp.sum(p, axis=-1, keepdims=True) + jnp.exp(sink - m))
    out = jnp.einsum("bnkgqs,bnskd->bnqkgd", probs.astype(v.dtype), vb)
    return out.reshape(B, S, N_HEADS * HEAD_DIM)


def setup_inputs(seed: int = 0) -> dict:
    key = jax.random.key(seed)
    ks = jax.random.split(key, 24)
    f32 = jnp.float32

    def w(k, shape, fan_in):
        return jax.random.normal(k, shape, f32) * (fan_in ** -0.5)

    def gain(k, n):
        return 1.0 + 0.02 * jax.random.normal(k, (DEPTH, n), f32)

    def small(k, shape):
        return 0.02 * jax.random.normal(k, shape, f32)

    x = jax.random.normal(ks[0], (BATCH, SEQ, D_MODEL), f32)
    positions = jnp.broadcast_to(jnp.arange(SEQ, dtype=jnp.int32)[None, :], (BATCH, SEQ))
    return {
        "x": x,
        "positions": positions,
        "ffn1_norm": gain(ks[1], D_MODEL),
        "ffn1_w_gate": w(ks[2], (DEPTH, D_MODEL, D_FF), D_MODEL),
        "ffn1_w_up": w(ks[3], (DEPTH, D_MODEL, D_FF), D_MODEL),
        "ffn1_w_down": w(ks[4], (DEPTH, D_FF, D_MODEL), D_FF),
        "mix_norm": gain(ks[5], D_MODEL),
        "w_in": w(ks[6], (DEPTH, D_MODEL, D_IN), D_MODEL),
        "conv_dw_w": w(ks[7], (DEPTH, CONV_WIDTH, D_CONV), CONV_WIDTH),
        "conv_dw_b": small(ks[8], (DEPTH, D_CONV)),
        "conv_ln_g": gain(ks[9], D_CONV),
        "conv_ln_b": small(ks[10], (DEPTH, D_CONV)),
        "conv_w_proj": w(ks[11], (DEPTH, D_CONV, D_MODEL), D_CONV),
        "attn_sinks": 0.5 * jax.random.normal(ks[12], (DEPTH, N_HEADS), f32),
        "attn_w_o": w(ks[13], (DEPTH, N_HEADS * HEAD_DIM, D_MODEL), N_HEADS * HEAD_DIM),
        "gate_b": small(ks[14], (DEPTH, 2 * D_MODEL)),
        "w_out": w(ks[15], (DEPTH, D_MODEL, D_MODEL), D_MODEL),
        "ffn2_norm": gain(ks[16], D_MODEL),
        "ffn2_w_gate": w(ks[17], (DEPTH, D_MODEL, D_FF), D_MODEL),
        "ffn2_w_up": w(ks[18], (DEPTH, D_MODEL, D_FF), D_MODEL),
        "ffn2_w_down": w(ks[19], (DEPTH, D_FF, D_MODEL), D_FF),
        "final_norm": 1.0 + 0.02 * jax.random.normal(ks[20], (D_MODEL,), f32),
    }


def reference(x, positions, ffn1_norm, ffn1_w_gate, ffn1_w_up, ffn1_w_down, mix_norm, w_in,
              conv_dw_w, conv_dw_b, conv_ln_g, conv_ln_b, conv_w_proj, attn_sinks, attn_w_o,
              gate_b, w_out, ffn2_norm, ffn2_w_gate, ffn2_w_up, ffn2_w_down, final_norm):
    B, S, _ = x.shape
    bounds = np.cumsum(SPLITS)[:-1].tolist()
    for l in range(DEPTH):
        x = x + 0.5 * swiglu(rmsnorm(x, ffn1_norm[l]), ffn1_w_gate[l], ffn1_w_up[l], ffn1_w_down[l])

        h = rmsnorm(x, mix_norm[l])
        proj = h @ w_in[l]
        glu_a, glu_b, q, k, v, g_conv, g_attn = jnp.split(proj, bounds, axis=-1)

        conv_out = conformer_conv_branch(glu_a, glu_b, conv_dw_w[l], conv_dw_b[l],
                                         conv_ln_g[l], conv_ln_b[l], conv_w_proj[l])

        q = rope(q.reshape(B, S, N_HEADS, HEAD_DIM), positions)
        k = rope(k.reshape(B, S, N_KV_HEADS, HEAD_DIM), positions)
        v = v.reshape(B, S, N_KV_HEADS, HEAD_DIM)
        attn_out = sliding_window_gqa_sinks(q, k, v, attn_sinks[l]) @ attn_w_o[l]

        gb_conv, gb_attn = jnp.split(gate_b[l], 2)
        merged = (jax.nn.sigmoid(g_conv + gb_conv) * conv_out
                  + jax.nn.sigmoid(g_attn + gb_attn) * attn_out)
        x = x + merged @ w_out[l]

        x = x + 0.5 * swiglu(rmsnorm(x, ffn2_norm[l]), ffn2_w_gate[l], ffn2_w_up[l], ffn2_w_down[l])
    return rmsnorm(x, final_norm)
```

```python
import numpy as np
import concourse.bass as bass
import concourse.mybir as mybir
from concourse.bass_utils import run_bass_kernel_spmd
from contextlib import ExitStack

F32, BF16, I32 = mybir.dt.float32, mybir.dt.bfloat16, mybir.dt.int32
AF = mybir.ActivationFunctionType
ALU = mybir.AluOpType

D = 1024; DFF = 2816; NFC = 22; SEQ = 8192; BATCH = 2
TOK = 2048; HALO = 128; TL = TOK + HALO; NT = 512
CW = 31; EPS = 1e-6; LN_EPS = 1e-5
NEG = -30000.0
PI = float(np.pi); TWO_PI = float(2 * np.pi)
NBLK_A = 78
SAME_ENGINE_SYNC = True
DEBUG = False

C_FFN1, C_MIX, C_FFN2, C_FIN, C_DWB, C_LNG, C_LNB, C_GBC, C_GBA = range(9)


class Sem:
    def __init__(self, h):
        self.h = h
        self.n = 0


HIST = {}
NWAIT = [0, 0]


class Eng:
    def __init__(self, e, sem):
        self.e = e
        self.sem = sem
        self.known = {}
        self.no_self = False
        self._snap = None

    def snap(self):
        if self._snap is None:
            self._snap = dict(self.known)
        return self._snap

    def learn(self, sem, val):
        if self.known.get(sem, 0) < val:
            self.known[sem] = val
            self._snap = None
        h = HIST.get((sem, val))
        if h:
            for s2, v2 in h.items():
                if self.known.get(s2, 0) < v2:
                    self.known[s2] = v2
                    self._snap = None

    def wait(self, sem, val):
        if val <= 0:
            return
        if sem is self.sem:
            if val > sem.n or not SAME_ENGINE_SYNC or self.no_self:
                return
        if self.known.get(sem, 0) >= val:
            NWAIT[1] += 1
            return
        self.e.wait_ge(sem.h, val)
        NWAIT[0] += 1
        self.learn(sem, val)


def _prune(deps):
    items = list(deps.items())
    keep = dict(deps)
    for s, v in items:
        h = HIST.get((s, v))
        if not h or s not in keep:
            continue
        for s2, v2 in items:
            if s2 is not s and s2 in keep and h.get(s2, 0) >= v2:
                del keep[s2]
    return keep


class Buf:
    __slots__ = ("w", "r", "psum", "indep")

    def __init__(self, psum=False, indep=False):
        self.w = {}
        self.r = {}
        self.psum = psum
        self.indep = indep


def _deps(rd, wr, me=None):
    deps = {}
    for b in rd:
        for s, v in b.w.items():
            if deps.get(s, 0) < v:
                deps[s] = v
        if b.psum:
            for s, v in b.r.items():
                if s is not me and deps.get(s, 0) < v:
                    deps[s] = v
    for b in wr:
        for s, v in b.w.items():
            if (s is me and b.indep):
                continue
            if deps.get(s, 0) < v:
                deps[s] = v
        for s, v in b.r.items():
            if deps.get(s, 0) < v:
                deps[s] = v
    return deps


def op(eng, fn, rd=(), wr=()):
    for s, v in _prune(_deps(rd, wr, eng.sem)).items():
        eng.wait(s, v)
    inst = fn()
    eng.sem.n += 1
    inst.then_inc(eng.sem.h, 1)
    ev = eng.sem.n
    HIST[(eng.sem, ev)] = eng.snap()
    for b in wr:
        b.w = {eng.sem: ev}
        b.r = {}
    for b in rd:
        if b.r.get(eng.sem, 0) < ev:
            b.r[eng.sem] = ev


def dma(q, dsem, out_ap, in_ap, rd=(), wr=()):
    for s, v in _prune(_deps(rd, wr, None)).items():
        q.wait(s, v)
    q.wait(dsem, dsem.n)
    inst = q.e.dma_start(out=out_ap, in_=in_ap)
    dsem.n += 16
    inst.then_inc(dsem.h, 16)
    HIST[(dsem, dsem.n)] = q.snap()
    for b in wr:
        b.w = {dsem: dsem.n}
        b.r = {}
    for b in rd:
        b.r[dsem] = dsem.n


class Stream:
    def __init__(self, q, slots, bufs, sems, keys, src, live=2):
        self.live = live
        self.q, self.slots, self.bufs, self.sems, self.keys, self.src = q, slots, bufs, sems, keys, src
        self.depth = len(slots)
        self.i = 0
        self.issued = 0
        self.rec = []
        if keys is not None:
            for _ in range(self.depth - self.live):
                self._issue()

    def _issue(self):
        j = self.issued
        if j >= len(self.keys):
            return
        s = j % self.depth
        dma(self.q, self.sems[s], self.slots[s], self.src(self.keys[j]), wr=(self.bufs[s],))
        self.issued += 1

    def next(self, key):
        i = self.i
        self.i += 1
        s = i % self.depth
        if self.keys is None:
            self.rec.append(key)
        else:
            assert self.keys[i] == key, (self.keys[i], key)
            self._issue()
        return self.slots[s], self.bufs[s]


def build_nc():
    _, rec = _build(None)
    nc, _ = _build(rec)
    return nc


def _build(rec):
    HIST.clear()
    NWAIT[0] = NWAIT[1] = 0
    nc = bass.Bass("TRN2", target_bir_lowering=False)
    dt = nc.dram_tensor
    xT_d = dt("xT", [D, TL], F32, kind="ExternalInput").ap()
    pos_d = dt("pos", [128, TL], I32, kind="ExternalInput").ap()
    masks_d = dt("masks", [128, 3, 512], F32, kind="ExternalInput").ap()
    wall_d = dt("wall", [D, NBLK_A * 256], F32, kind="ExternalInput").ap()
    wd_d = dt("wd", [2 * DFF, D], F32, kind="ExternalInput").ap()
    cols_d = dt("cols", [128, 9, 8], F32, kind="ExternalInput").ap()
    w2_d = dt("w2", [8, 128, 2048], F32, kind="ExternalInput").ap()
    misc_d = dt("misc", [128, 16], F32, kind="ExternalInput").ap()
    ident_d = dt("ident", [128, 128], F32, kind="ExternalInput").ap()
    pswap_d = dt("pswap", [128, 128], F32, kind="ExternalInput").ap()
    outT_d = dt("outT", [D, TOK], F32, kind="ExternalOutput").ap()
    dbg_d = None
    if DEBUG:
        dbg_d = dt("dbg", [6, D, NT], F32, kind="ExternalOutput").ap()

    with ExitStack() as es:
        def sb(name, shape, dtype):
            return es.enter_context(nc.sbuf_tensor(name, shape, dtype))

        def mksem(name):
            return Sem(es.enter_context(nc.semaphore(name)))

        PE = Eng(nc.tensor, mksem("s_pe"))
        PE.no_self = True
        ACT = Eng(nc.scalar, mksem("s_act"))
        DVE = Eng(nc.vector, mksem("s_dve"))
        POOL = Eng(nc.gpsimd, mksem("s_pool"))
        SP = Eng(nc.sync, mksem("s_sp"))

        X = [sb(f"X{i}", [128, 8, NT], F32) for i in range(2)]; X_b = [[Buf() for _ in range(8)] for _ in range(2)]
        xT, xT_b = X[0], X_b[0]
        cvo, cvo_b = X[1], X_b[1]
        xn = sb("xn", [128, 8, NT], BF16); xn_b = [Buf() for _ in range(8)]
        Hh = sb("Hh", [128, NFC, NT], BF16); H_b = [Buf() for _ in range(NFC)]
        NTMP = 5
        tmp = [sb(f"tmp{i}", [128, NT], F32) for i in range(NTMP)]; tmp_b = [Buf() for _ in range(NTMP)]
        NSQ = 4
        sq = [sb(f"sq{i}", [128, NT], BF16) for i in range(NSQ)]; sq_b = [Buf() for _ in range(NSQ)]
        NA = 6
        wa = [sb(f"wa{i}", [128, 8, 256], BF16) for i in range(NA)]; wa_b = [Buf() for _ in range(NA)]
        wa_s = [mksem(f"s_wa{i}") for i in range(NA)]
        ND = 6
        wdd = [sb(f"wd{i}", [128, 2, 512], BF16) for i in range(ND)]; wd_b = [Buf() for _ in range(ND)]
        wd_s = [mksem(f"s_wd{i}") for i in range(ND)]
        uT = sb("uT", [128, 8, 30 + NT], BF16); uT_b = [Buf() for _ in range(8)]
        RR = [[sb(f"RR{i}{g}", [128, 30 + NT], BF16) for g in range(2)] for i in range(2)]
        RR_b = [[Buf(indep=True) for g in range(2)] for i in range(2)]
        cT = sb("cT", [128, 8, NT], BF16); cT_b = [Buf() for _ in range(8)]
        qT = sb("qT", [128, 8, NT], BF16); qT_b = [Buf() for _ in range(8)]
        aT = sb("aT", [128, 8, NT], BF16); aT_b = [[Buf() for _ in range(4)] for _ in range(2)]
        mg, mg_b = qT, qT_b
        kTz = sb("kTz", [128, 4, 8 * 128], BF16); kTz_b = [[Buf() for _ in range(8)] for _ in range(4)]
        vz = sb("vz", [128, 8, 512], BF16); vz_b = [Buf() for _ in range(8)]
        cos_t = sb("cos_t", [128, NT], F32); cos_b = Buf()
        sin_t = sb("sin_t", [128, NT], F32); sin_b = Buf()
        posi = sb("posi", [128, NT], I32); posi_b = Buf()
        NPT = 6
        PT = [sb(f"PT{i}", [128, 512], BF16) for i in range(NPT)]; PT_b = [Buf() for _ in range(NPT)]
        masks = sb("masks_sb", [128, 3, 512], BF16); masks_b = Buf()
        sinkt = sb("sinkt", [128, 2, 512], F32); sinkt_b = Buf()
        ident = sb("ident_sb", [128, 128], BF16); ident_b = Buf()
        pswap = sb("pswap_sb", [128, 128], BF16); pswap_b = Buf()
        ones = sb("ones_sb", [128, 128], BF16); ones_b = Buf()
        onesz = sb("onesz", [128, 2, 128], BF16); onesz_b = Buf()
        cols = sb("cols_sb", [128, 9, 8], F32); cols_b = Buf()
        misc = sb("misc_sb", [128, 16], F32); misc_b = Buf()
        Xh = sb("Xh", [128, 8, HALO], F32); Xh_b = [Buf() for _ in range(8)]
        xnh = sb("xnh", [128, 8, HALO], BF16); xnh_b = [Buf() for _ in range(8)]
        Hhh = sb("Hhh", [128, NFC, HALO], BF16); Hhh_b = [Buf() for _ in range(NFC)]
        cos_h = sb("cos_h", [128, HALO], F32); cosh_b = Buf()
        sin_h = sb("sin_h", [128, HALO], F32); sinh_b = Buf()
        scr = sb("scr", [128, 2], F32); scr_b = Buf()
        s_x = mksem("s_x")
        s_pos = mksem("s_pos")
        s_out = mksem("s_out")
        s_out2 = mksem("s_out2")

        ps = [es.enter_context(nc.psum_tensor(f"ps{i}", [128, NT], F32)) for i in range(8)]
        ps_b = [Buf(psum=True) for _ in range(8)]
        ps_ctr = [0]

        def bank():
            i = ps_ctr[0] % 6
            ps_ctr[0] += 1
            return ps[i], ps_b[i]

        def stat_bank(i):
            return ps[6 + i], ps_b[6 + i]

        tmp_ctr = [0]

        def gettmp():
            i = tmp_ctr[0] % NTMP
            tmp_ctr[0] += 1
            return tmp[i], tmp_b[i]

        sq_ctr = [0]

        def getsq():
            i = sq_ctr[0] % NSQ
            sq_ctr[0] += 1
            return sq[i], sq_b[i]

        pt_ctr = [0]

        def getpt():
            i = pt_ctr[0] % NPT
            pt_ctr[0] += 1
            return PT[i], PT_b[i]

        def cdma(q, name, dst, src, buf):
            dma(q, mksem("s_c_" + name), dst, src, wr=(buf,))

        cdma(SP, "cols", cols[:], cols_d, cols_b)
        cdma(SP, "misc", misc[:], misc_d, misc_b)
        op(DVE, lambda: nc.vector.memset(ones[:], 1.0), wr=(ones_b,))

        def late_setup():
            cdma(POOL, "ident", ident[:], ident_d, ident_b)
            cdma(POOL, "pswap", pswap[:], pswap_d, pswap_b)
            cdma(POOL, "masks", masks[:], masks_d, masks_b)
            op(DVE, lambda: nc.vector.memset(onesz[:], 0.0), wr=(onesz_b,))
            op(DVE, lambda: nc.vector.memset(onesz[:, 0, 0:64], 1.0), wr=(onesz_b,))
            op(DVE, lambda: nc.vector.memset(onesz[:, 1, 64:128], 1.0), wr=(onesz_b,))
            allk = [b_ for g_ in kTz_b for b_ in g_]
            op(DVE, lambda: nc.vector.memset(kTz[:], 0.0), wr=allk)
            op(DVE, lambda: nc.vector.memset(vz[:], 0.0), wr=vz_b)
            op(DVE, lambda: nc.vector.memset(uT[:], 0.0), wr=uT_b)
            for i_ in range(2):
                for g_ in range(2):
                    op(DVE, lambda i_=i_, g_=g_: nc.vector.memset(RR[i_][g_][:], 0.0), wr=(RR_b[i_][g_],))
            for gp in range(2):
                for i in range(4):
                    op(ACT, lambda gp=gp, i=i: nc.scalar.activation(
                        out=sinkt[:, gp, i * 128:(i + 1) * 128], in_=ident[:], func=AF.Exp,
                        bias=misc[:, 2 + gp * 4 + i:3 + gp * 4 + i], scale=0.0),
                       rd=(ident_b, misc_b), wr=(sinkt_b,))

        def ablk(j):
            return wall_d[:, j * 256:(j + 1) * 256].rearrange("(kc p) j -> p kc j", p=128)

        def dblk(f, h, b):
            r0 = f * DFF + b * 256
            return wd_d[r0:r0 + 256, h * 512:(h + 1) * 512].rearrange("(j p) c -> p j c", p=128)

        def asrc(key):
            if key[0] == "W":
                return w2_d[key[1]].rearrange("p (a b) -> p a b", a=8)
            return ablk(key[1])

        def dsrc(key):
            return dblk(key[1], key[2], key[3])

        SA = Stream(POOL, [w[:] for w in wa], wa_b, wa_s, None if rec is None else rec[0], asrc)
        SD = Stream(POOL, [w[:] for w in wdd], wd_b, wd_s, None if rec is None else rec[1], dsrc, live=1)

        def proj(pst, pb, wslot, wbuf, sub, rhs_t, rhs_bufs, N):
            for kc in range(8):
                op(PE, lambda kc=kc: nc.tensor.matmul(
                    pst[:, 0:N], lhsT=wslot[:, kc, sub * 128:(sub + 1) * 128], rhs=rhs_t[:, kc, 0:N],
                    start=(kc == 0), stop=(kc == 7)),
                   rd=(wbuf, rhs_bufs[kc]), wr=(pb,))

        def rmsnorm(N, gcol, out_t, out_bufs, x_t=None, x_bufs=None):
            if x_t is None:
                x_t, x_bufs = xT, xT_b
            pst, pb = stat_bank(0)
            op(ACT, lambda: nc.scalar.activation(out=scr[:, 0:1], in_=misc[:, 10:11], func=AF.Ln), rd=(misc_b,), wr=(scr_b,))
            for c in range(8):
                s_t, s_b = getsq()
                op(ACT, lambda c=c, s_t=s_t: nc.scalar.activation(out=s_t[:, 0:N], in_=x_t[:, c, 0:N], func=AF.Square),
                   rd=(x_bufs[c],), wr=(s_b,))
                op(PE, lambda c=c, s_t=s_t: nc.tensor.matmul(pst[:, 0:N], lhsT=ones[:], rhs=s_t[:, 0:N],
                                                             start=(c == 0), stop=(c == 7)),
                   rd=(ones_b, s_b), wr=(pb,))
            t_t, t_b = gettmp()
            op(ACT, lambda: nc.scalar.activation(out=t_t[:, 0:N], in_=pst[:, 0:N], func=AF.Ln, scale=1.0 / D, bias=eps_col(EPS)),
               rd=(pb, misc_b), wr=(t_b,))
            op(ACT, lambda: nc.scalar.activation(out=pst[:, 0:N], in_=t_t[:, 0:N], func=AF.Exp, scale=-0.5),
               rd=(t_b,), wr=(pb,))
            for c in range(8):
                op(DVE, lambda c=c: nc.vector.scalar_tensor_tensor(
                    out=out_t[:, c, 0:N], in0=x_t[:, c, 0:N], scalar=cols[:, gcol, c:c + 1], in1=pst[:, 0:N],
                    op0=ALU.mult, op1=ALU.mult),
                   rd=(x_bufs[c], cols_b, pb), wr=(out_bufs[c],))

        def eps_col(v):
            return misc[:, 10:11] if v == EPS else misc[:, 11:12]

        def ffn_gu(f, N, halo=False):
            for b in range(11):
                wg, wgb = SA.next(("A", (56 if f else 0) + 2 * b))
                wu, wub = SA.next(("A", (56 if f else 0) + 2 * b + 1))
                for sub in range(2):
                    fc = 2 * b + sub
                    pg, pgb = bank()
                    pu, pub = bank()
                    proj(pg, pgb, wg, wgb, sub, xn, xn_b, N)
                    proj(pu, pub, wu, wub, sub, xn, xn_b, N)
                    t_t, t_b = gettmp()
                    op(ACT, lambda: nc.scalar.activation(out=t_t[:, 0:N], in_=pg[:, 0:N], func=AF.Silu),
                       rd=(pgb,), wr=(t_b,))
                    op(DVE, lambda: nc.vector.tensor_tensor(out=Hh[:, fc, 0:N], in0=pu[:, 0:N], in1=t_t[:, 0:N], op=ALU.mult),
                       rd=(pub, t_b), wr=(H_b[fc],))
                    if halo:
                        pgh, pghb = bank()
                        puh, puhb = bank()
                        proj(pgh, pghb, wg, wgb, sub, xnh, xnh_b, HALO)
                        proj(puh, puhb, wu, wub, sub, xnh, xnh_b, HALO)
                        th, thb = gettmp()
                        op(ACT, lambda: nc.scalar.activation(out=th[:, 0:HALO], in_=pgh[:, 0:HALO], func=AF.Silu),
                           rd=(pghb,), wr=(thb,))
                        op(DVE, lambda: nc.vector.tensor_tensor(out=Hhh[:, fc, :], in0=puh[:, 0:HALO], in1=th[:, 0:HALO], op=ALU.mult),
                           rd=(puhb, thb), wr=(Hhh_b[fc],))

        def ffn_down(f, N, halo=False):
            for h in range(2):
                banks = [bank() for _ in range(4)]
                if halo:
                    hb, hbb = bank()
                for b in range(11):
                    wdt, wdb = SD.next(("D", f, h, b))
                    for sub in range(2):
                        fc = 2 * b + sub
                        for dcl in range(4):
                            pst, pb = banks[dcl]
                            op(PE, lambda pst=pst, dcl=dcl: nc.tensor.matmul(
                                pst[:, 0:N], lhsT=wdt[:, sub, dcl * 128:(dcl + 1) * 128], rhs=Hh[:, fc, 0:N],
                                start=(fc == 0), stop=(fc == NFC - 1)),
                               rd=(wdb, H_b[fc]), wr=(pb,))
                        if halo:
                            for dcl in range(4):
                                op(PE, lambda dcl=dcl: nc.tensor.matmul(
                                    hb[:, dcl * HALO:(dcl + 1) * HALO], lhsT=wdt[:, sub, dcl * 128:(dcl + 1) * 128], rhs=Hhh[:, fc, :],
                                    start=(fc == 0 and dcl == 0), stop=(fc == NFC - 1 and dcl == 3), skip_group_check=True),
                                   rd=(wdb, Hhh_b[fc]), wr=(hbb,))
                for dcl in range(4):
                    dc = h * 4 + dcl
                    pst, pb = banks[dcl]
                    op(DVE, lambda pst=pst, dc=dc: nc.vector.scalar_tensor_tensor(
                        out=xT[:, dc, 0:N], in0=pst[:, 0:N], scalar=0.5, in1=xT[:, dc, 0:N],
                        op0=ALU.mult, op1=ALU.add),
                       rd=(pb, xT_b[dc]), wr=(xT_b[dc],))
                if halo:
                    for dcl in range(4):
                        dc = h * 4 + dcl
                        op(DVE, lambda dcl=dcl, dc=dc: nc.vector.scalar_tensor_tensor(
                            out=Xh[:, dc, :], in0=hb[:, dcl * HALO:(dcl + 1) * HALO], scalar=0.5, in1=Xh[:, dc, :],
                            op0=ALU.mult, op1=ALU.add),
                           rd=(hbb, Xh_b[dc]), wr=(Xh_b[dc],))

        def rope_tables(base, N):
            dma(SP, s_pos, posi[:, 0:N], pos_d[:, base:base + N], wr=(posi_b,))
            a_t, a_b = gettmp()
            r_t, r_b = gettmp()
            w_t, w_b = gettmp()
            op(DVE, lambda: nc.vector.tensor_copy(out=a_t[:, 0:N], in_=posi[:, 0:N]), rd=(posi_b,), wr=(a_b,))
            op(DVE, lambda: nc.vector.tensor_scalar(out=a_t[:, 0:N], in0=a_t[:, 0:N], scalar1=misc[:, 0:1], scalar2=None,
                                                    op0=ALU.mult), rd=(a_b, misc_b), wr=(a_b,))
            op(DVE, lambda: nc.vector.tensor_scalar(out=posi[:, 0:N], in0=a_t[:, 0:N], scalar1=1.0 / TWO_PI, scalar2=None,
                                                    op0=ALU.mult), rd=(a_b,), wr=(posi_b,))
            op(DVE, lambda: nc.vector.tensor_copy(out=r_t[:, 0:N], in_=posi[:, 0:N]), rd=(posi_b,), wr=(r_b,))
            op(DVE, lambda: nc.vector.scalar_tensor_tensor(out=r_t[:, 0:N], in0=r_t[:, 0:N], scalar=-TWO_PI, in1=a_t[:, 0:N],
                                                           op0=ALU.mult, op1=ALU.add), rd=(r_b, a_b), wr=(r_b,))
            op(DVE, lambda: nc.vector.tensor_scalar(out=w_t[:, 0:N], in0=r_t[:, 0:N], scalar1=PI, scalar2=-TWO_PI,
                                                    op0=ALU.is_gt, op1=ALU.mult), rd=(r_b,), wr=(w_b,))
            op(DVE, lambda: nc.vector.tensor_tensor(out=r_t[:, 0:N], in0=r_t[:, 0:N], in1=w_t[:, 0:N], op=ALU.add),
               rd=(r_b, w_b), wr=(r_b,))
            op(DVE, lambda: nc.vector.tensor_scalar(out=r_t[:, 0:N], in0=r_t[:, 0:N], scalar1=-PI, scalar2=PI,
                                                    op0=ALU.max, op1=ALU.min), rd=(r_b,), wr=(r_b,))
            op(ACT, lambda: nc.scalar.activation(out=sin_t[:, 0:N], in_=r_t[:, 0:N], func=AF.Sin, scale=misc[:, 1:2]),
               rd=(r_b, misc_b), wr=(sin_b,))
            op(DVE, lambda: nc.vector.scalar_tensor_tensor(out=w_t[:, 0:N], in0=r_t[:, 0:N], scalar=-1.0, in1=r_t[:, 0:N],
                                                           op0=ALU.mult, op1=ALU.max), rd=(r_b,), wr=(w_b,))
            op(ACT, lambda: nc.scalar.activation(out=cos_t[:, 0:N], in_=w_t[:, 0:N], func=AF.Sin, scale=-1.0, bias=misc[:, 12:13]),
               rd=(w_b, misc_b), wr=(cos_b,))

        def rope_pipeline(chunks, N, hook=None):
            LOOKR = 2
            queue = []

            def finish(item):
                pa_, pab_, qb_, qbb_, done_ = item
                pr, prb = bank()
                op(PE, lambda: nc.tensor.matmul(pr[:, 0:N], lhsT=pswap[:], rhs=qb_[:, 0:N], start=True, stop=True),
                   rd=(pswap_b, qbb_), wr=(prb,))
                t1, t1b = gettmp()
                t2, t2b = gettmp()
                op(DVE, lambda: nc.vector.tensor_tensor(out=t1[:, 0:N], in0=pa_[:, 0:N], in1=cos_t[:, 0:N], op=ALU.mult),
                   rd=(pab_, cos_b), wr=(t1b,))
                op(DVE, lambda: nc.vector.tensor_tensor(out=t2[:, 0:N], in0=pr[:, 0:N], in1=sin_t[:, 0:N], op=ALU.mult),
                   rd=(prb, sin_b), wr=(t2b,))
                done_(t1, t1b, t2, t2b)

            for ch in chunks:
                wslot, wbuf, sub, done = ch
                if callable(wslot):
                    wslot, wbuf = wslot()
                pa, pab = bank()
                proj(pa, pab, wslot, wbuf, sub, xn, xn_b, N)
                qb, qbb = getsq()
                op(ACT, lambda: nc.scalar.activation(out=qb[:, 0:N], in_=pa[:, 0:N], func=AF.Copy), rd=(pab,), wr=(qbb,))
                queue.append((pa, pab, qb, qbb, done))
                if hook is not None:
                    hook()
                    hook = None
                if len(queue) > LOOKR:
                    finish(queue.pop(0))
            while queue:
                finish(queue.pop(0))

        def mixer(tile_idx, base, N, full, do_norm=True):
            nblk = N // 128
            gb0 = base // 128
            if do_norm:
                rmsnorm(N, C_MIX, xn, xn_b)

            wk, wkb = SA.next(("A", 22))

            def k_done(gp):
                def done(t1, t1b, t2, t2b):
                    for e in range(2):
                        g = 2 * gp + e
                        for bl in range(nblk):
                            slot = (gb0 + bl) % 8
                            op(DVE, lambda: nc.vector.tensor_tensor(
                                out=kTz[64 * e:64 * e + 64, g, slot * 128:(slot + 1) * 128],
                                in0=t1[64 * e:64 * e + 64, bl * 128:(bl + 1) * 128],
                                in1=t2[64 * e:64 * e + 64, bl * 128:(bl + 1) * 128], op=ALU.add),
                               rd=(t1b, t2b), wr=(kTz_b[g][slot],))
                return done
            rope_pipeline([(wk, wkb, gp, k_done(gp)) for gp in range(2)], N)
            wv, wvb = SA.next(("A", 23))
            for bl in range(nblk):
                slot = (gb0 + bl) % 8
                pst, pb = bank()
                for kc in range(8):
                    op(PE, lambda kc=kc, bl=bl, pst=pst: nc.tensor.matmul(
                        pst[:, 0:256], lhsT=xn[:, kc, bl * 128:(bl + 1) * 128], rhs=wv[:, kc, 0:256],
                        start=(kc == 0), stop=(kc == 7)),
                       rd=(wvb, xn_b[kc]), wr=(pb,))
                for e in range(2):
                    op(ACT, lambda e=e, slot=slot, pst=pst: nc.scalar.activation(
                        out=vz[:, slot, :].rearrange("p (gp r) -> p gp r", gp=2)[:, :, e * 192:e * 192 + 64],
                        in_=pst[:, 0:256].rearrange("p (gp e d) -> p gp e d", gp=2, e=2)[:, :, e, :],
                        func=AF.Copy),
                       rd=(pb,), wr=(vz_b[slot],))
            s1p = s2p = None
            if full:
                s1p, s1b = stat_bank(0)
                s2p, s2b = stat_bank(1)
            pending = []

            def emit_glu(cc):
                wgl, wglb = SA.next(("A", 28 + cc))
                pa, pab = bank()
                pbk, pbb = bank()
                proj(pa, pab, wgl, wglb, 0, xn, xn_b, N)
                proj(pbk, pbb, wgl, wglb, 1, xn, xn_b, N)
                t_t, t_b = gettmp()
                op(ACT, lambda: nc.scalar.activation(out=t_t[:, 0:N], in_=pbk[:, 0:N], func=AF.Sigmoid),
                   rd=(pbb,), wr=(t_b,))
                op(DVE, lambda: nc.vector.tensor_tensor(out=uT[:, cc, 30:30 + N], in0=pa[:, 0:N], in1=t_t[:, 0:N], op=ALU.mult),
                   rd=(pab, t_b), wr=(uT_b[cc],))
                if full:
                    W_ = 30 + N
                    for g2 in range(2):
                        rr, rrb = RR[cc % 2][g2], RR_b[cc % 2][g2]
                        lo, hi = 64 * g2, 64 * g2 + 64
                        op(DVE, lambda: nc.vector.tensor_copy(out=rr[0:64, 0:W_], in_=uT[lo:hi, cc, 0:W_]),
                           rd=(uT_b[cc],), wr=(rrb,))
                        op(DVE, lambda: nc.vector.tensor_copy(out=rr[64:128, 0:W_ - 1], in_=uT[lo:hi, cc, 1:W_]),
                           rd=(uT_b[cc],), wr=(rrb,))

            def emit_conv(cc):
                nonlocal pending
                if full:
                    dg, dgb = SA.next(("W", cc))
                    pc, pcb = bank()
                    for j in range(16):
                        for g2 in range(2):
                            op(PE, lambda j=j, g2=g2: nc.tensor.matmul(
                                pc[64 * g2:64 * g2 + 64, 0:N], lhsT=dg[:, 4 * g2 + j // 4, (j % 4) * 64:(j % 4) * 64 + 64],
                                rhs=RR[cc % 2][g2][:, 2 * j:2 * j + N], start=(j == 0), stop=(j == 15)),
                               rd=(dgb, RR_b[cc % 2][g2]), wr=(pcb,))
                    for f_ in pending:
                        f_()
                    pending = []
                    op(DVE, lambda: nc.vector.tensor_scalar(out=cvo[:, cc, 0:N], in0=pc[:, 0:N], scalar1=cols[:, C_DWB, cc:cc + 1],
                                                            scalar2=None, op0=ALU.add),
                       rd=(pcb, cols_b), wr=(cvo_b[cc],))
                    a_t, a_b = getsq()
                    op(DVE, lambda: nc.vector.tensor_copy(out=a_t[:, 0:N], in_=cvo[:, cc, 0:N]),
                       rd=(cvo_b[cc],), wr=(a_b,))
                    q_t, q_b = getsq()
                    op(ACT, lambda: nc.scalar.activation(out=q_t[:, 0:N], in_=cvo[:, cc, 0:N], func=AF.Square),
                       rd=(cvo_b[cc],), wr=(q_b,))

                    def stats(cc=cc, a_t=a_t, a_b=a_b, q_t=q_t, q_b=q_b):
                        op(PE, lambda: nc.tensor.matmul(s1p[:, 0:N], lhsT=ones[:], rhs=a_t[:, 0:N], start=(cc == 0), stop=(cc == 7)),
                           rd=(ones_b, a_b), wr=(s1b,))
                        op(PE, lambda: nc.tensor.matmul(s2p[:, 0:N], lhsT=ones[:], rhs=q_t[:, 0:N], start=(cc == 0), stop=(cc == 7)),
                           rd=(ones_b, q_b), wr=(s2b,))
                    pending.append(stats)
                op(ACT, lambda: nc.scalar.activation(out=uT[:, cc, 0:30], in_=uT[:, cc, N:N + 30], func=AF.Copy),
                   rd=(uT_b[cc],), wr=(uT_b[cc],))

            emit_glu(0)
            for cc in range(8):
                if cc + 1 < 8:
                    emit_glu(cc + 1)
                emit_conv(cc)
            if not full:
                return

            def ln_prologue():
                m_t, m_b = gettmp()
                v_t, v_b = gettmp()
                r_t, r_b = gettmp()
                op(ACT, lambda: nc.scalar.activation(out=m_t[:, 0:N], in_=s1p[:, 0:N], func=AF.Square, scale=1.0 / D),
                   rd=(s1b,), wr=(m_b,))
                op(DVE, lambda: nc.vector.scalar_tensor_tensor(out=v_t[:, 0:N], in0=s2p[:, 0:N], scalar=1.0 / D, in1=m_t[:, 0:N],
                                                               op0=ALU.mult, op1=ALU.subtract), rd=(s2b, m_b), wr=(v_b,))
                op(ACT, lambda: nc.scalar.activation(out=v_t[:, 0:N], in_=v_t[:, 0:N], func=AF.Ln, bias=eps_col(LN_EPS)),
                   rd=(v_b, misc_b), wr=(v_b,))
                op(ACT, lambda: nc.scalar.activation(out=r_t[:, 0:N], in_=v_t[:, 0:N], func=AF.Exp, scale=-0.5),
                   rd=(v_b,), wr=(r_b,))
                op(DVE, lambda: nc.vector.scalar_tensor_tensor(out=m_t[:, 0:N], in0=s1p[:, 0:N], scalar=-1.0 / D, in1=r_t[:, 0:N],
                                                               op0=ALU.mult, op1=ALU.mult), rd=(s1b, r_b), wr=(m_b,))
                op(ACT, lambda: nc.scalar.activation(out=s2p[:, 0:N], in_=r_t[:, 0:N], func=AF.Copy),
                   rd=(r_b,), wr=(s2b,))
                op(ACT, lambda: nc.scalar.activation(out=s1p[:, 0:N], in_=m_t[:, 0:N], func=AF.Copy),
                   rd=(m_b,), wr=(s1b,))

            def ln_apply(cc):
                op(DVE, lambda: nc.vector.tensor_tensor(out=cvo[:, cc, 0:N], in0=cvo[:, cc, 0:N], in1=s2p[:, 0:N], op=ALU.mult),
                   rd=(cvo_b[cc], s2b), wr=(cvo_b[cc],))
                op(DVE, lambda: nc.vector.tensor_tensor(out=cvo[:, cc, 0:N], in0=cvo[:, cc, 0:N], in1=s1p[:, 0:N], op=ALU.add),
                   rd=(cvo_b[cc], s1b), wr=(cvo_b[cc],))
                op(ACT, lambda: nc.scalar.activation(out=cT[:, cc, 0:N], in_=cvo[:, cc, 0:N], func=AF.Silu,
                                                     scale=cols[:, C_LNG, cc:cc + 1], bias=cols[:, C_LNB, cc:cc + 1]),
                   rd=(cvo_b[cc], cols_b), wr=(cT_b[cc],))

            qw = {}

            def after_first_q():
                for f_ in pending:
                    f_()
                ln_prologue()

            def q_chunk(qc):
                def getw():
                    if qc % 2 == 0:
                        qw["w"] = SA.next(("A", 24 + qc // 2))
                    return qw["w"]

                def done(t1, t1b, t2, t2b):
                    op(POOL, lambda: nc.gpsimd.tensor_tensor(out=qT[:, qc, 0:N], in0=t1[:, 0:N], in1=t2[:, 0:N], op=ALU.add),
                       rd=(t1b, t2b), wr=(qT_b[qc],))
                    ln_apply(qc)
                return (getw, None, qc % 2, done)
            rope_pipeline([q_chunk(qc) for qc in range(8)], N, hook=after_first_q)
            pass
            steps = [(bl, gp, e) for bl in range(nblk) for gp in range(2) for e in range(2)]

            def emit_scores(step):
                bl, gp, e = step
                g = 2 * gp + e
                gblk = gb0 + bl
                res = []
                for kt, slot in ((0, (gblk - 1) % 8), (1, gblk % 8)):
                    pst, pb = bank()
                    op(PE, lambda: nc.tensor.matmul(
                        pst[:, :].rearrange("p (i q) -> p i q", i=4), lhsT=kTz[:, g, slot * 128:(slot + 1) * 128],
                        rhs=qT[:, gp * 4:(gp + 1) * 4, bl * 128:(bl + 1) * 128], start=True, stop=False),
                       rd=[kTz_b[g][slot]] + [qT_b[gp * 4 + i] for i in range(4)], wr=(pb,))
                    mi = 2 if kt == 1 else (1 if gblk == 1 else 0)
                    op(PE, lambda: nc.tensor.matmul(pst[:, :], lhsT=ident[:], rhs=masks[:, mi, :], start=False, stop=True),
                       rd=(ident_b, masks_b), wr=(pb,))
                    p_t, p_b = getpt()
                    op(ACT, lambda: nc.scalar.activation(out=p_t[:], in_=pst[:, :], func=AF.Exp, scale=0.125),
                       rd=(pb,), wr=(p_b,))
                    res.append((p_t, p_b, slot, g))
                return res

            LOOK = 2
            queue = [emit_scores(steps[i]) for i in range(min(LOOK, len(steps)))]
            pn, pnb = stat_bank(0)
            pd, pdb = stat_bank(1)
            for si, (bl, gp, e) in enumerate(steps):
                if si + LOOK < len(steps):
                    queue.append(emit_scores(steps[si + LOOK]))
                pend = queue.pop(0)
                for n_, (p_t, p_b, slot, g) in enumerate(pend):
                    op(PE, lambda: nc.tensor.matmul(pn[:, :], lhsT=vz[:, slot, g * 128:(g + 1) * 128], rhs=p_t[:],
                                                    start=(e == 0 and n_ == 0), stop=(e == 1 and n_ == 1)),
                       rd=(vz_b[slot], p_b), wr=(pnb,))
                for n_, (p_t, p_b, slot, g) in enumerate(pend):
                    op(PE, lambda: nc.tensor.matmul(pd[:, :], lhsT=onesz[:, e, :], rhs=p_t[:],
                                                    start=(e == 0 and n_ == 0), stop=(e == 1 and n_ == 1)),
                       rd=(onesz_b, p_b), wr=(pdb,))
                if e == 1:
                    d_t, d_b = gettmp()
                    n_t, n_b = gettmp()
                    op(DVE, lambda: nc.vector.tensor_tensor(out=d_t[:], in0=pd[:, :], in1=sinkt[:, gp, :], op=ALU.add),
                       rd=(pdb, sinkt_b), wr=(d_b,))
                    op(DVE, lambda: nc.vector.tensor_copy(out=n_t[:], in_=pn[:, :]), rd=(pnb,), wr=(n_b,))
                    op(ACT, lambda: nc.scalar.activation(out=d_t[:], in_=d_t[:], func=AF.Ln), rd=(d_b,), wr=(d_b,))
                    op(ACT, lambda: nc.scalar.activation(out=d_t[:], in_=d_t[:], func=AF.Exp, scale=-1.0), rd=(d_b,), wr=(d_b,))
                    op(DVE, lambda: nc.vector.tensor_tensor(
                        out=aT[:, gp * 4:(gp + 1) * 4, bl * 128:(bl + 1) * 128],
                        in0=n_t[:].rearrange("p (i q) -> p i q", i=4),
                        in1=d_t[:].rearrange("p (i q) -> p i q", i=4), op=ALU.mult),
                       rd=(n_b, d_b), wr=(aT_b[gp][bl],))
            aT_all = [aT_b[hc // 4] for hc in range(8)]
            for dc in range(8):
                wga, wgab = SA.next(("A", 36 + 2 * dc))
                wpb, wpbb = SA.next(("A", 37 + 2 * dc))
                pgc, pgcb = bank()
                pga, pgab = bank()
                pcp, pcpb = bank()
                pao, paob = bank()
                proj(pgc, pgcb, wga, wgab, 0, xn, xn_b, N)
                proj(pga, pgab, wga, wgab, 1, xn, xn_b, N)
                proj(pcp, pcpb, wpb, wpbb, 0, cT, cT_b, N)
                for hc in range(8):
                    op(PE, lambda hc=hc: nc.tensor.matmul(pao[:, 0:N], lhsT=wpb[:, hc, 128:256], rhs=aT[:, hc, 0:N],
                                                          start=(hc == 0), stop=(hc == 7)),
                       rd=[wpbb] + aT_all[hc], wr=(paob,))
                s1, s1bb = gettmp()
                s2, s2bb = gettmp()
                op(ACT, lambda: nc.scalar.activation(out=s1[:, 0:N], in_=pgc[:, 0:N], func=AF.Sigmoid, bias=cols[:, C_GBC, dc:dc + 1]),
                   rd=(pgcb, cols_b), wr=(s1bb,))
                op(ACT, lambda: nc.scalar.activation(out=s2[:, 0:N], in_=pga[:, 0:N], func=AF.Sigmoid, bias=cols[:, C_GBA, dc:dc + 1]),
                   rd=(pgab, cols_b), wr=(s2bb,))
                op(DVE, lambda: nc.vector.tensor_tensor(out=s1[:, 0:N], in0=pcp[:, 0:N], in1=s1[:, 0:N], op=ALU.mult),
                   rd=(pcpb, s1bb), wr=(s1bb,))
                op(DVE, lambda: nc.vector.tensor_tensor(out=s2[:, 0:N], in0=pao[:, 0:N], in1=s2[:, 0:N], op=ALU.mult),
                   rd=(paob, s2bb), wr=(s2bb,))
                op(DVE, lambda: nc.vector.tensor_tensor(out=mg[:, dc, 0:N], in0=s1[:, 0:N], in1=s2[:, 0:N], op=ALU.add),
                   rd=(s1bb, s2bb), wr=(mg_b[dc],))
            for j in range(4):
                wo, wob = SA.next(("A", 52 + j))
                for sub in range(2):
                    dc2 = 2 * j + sub
                    pst, pb = bank()
                    proj(pst, pb, wo, wob, sub, mg, mg_b, N)
                    op(DVE, lambda pst=pst, dc2=dc2: nc.vector.tensor_tensor(out=xT[:, dc2, 0:N], in0=pst[:, 0:N], in1=xT[:, dc2, 0:N], op=ALU.add),
                       rd=(pb, xT_b[dc2]), wr=(xT_b[dc2],))

        tiles = [(HALO + j * NT, NT, True) for j in range(4)]

        def load_and_norm1(ti):
            base, N, full = tiles[ti]
            x_t, x_bufs = X[ti % 2], X_b[ti % 2]
            dma(SP, s_x, x_t[:, :, 0:N], xT_d[:, base:base + N].rearrange("(c p) t -> p c t", p=128), wr=x_bufs)
            rmsnorm(N, C_FFN1, xn, xn_b, x_t, x_bufs)

        dma(SP, s_x, Xh[:], xT_d[:, 0:HALO].rearrange("(c p) t -> p c t", p=128), wr=Xh_b)
        rmsnorm(HALO, C_FFN1, xnh, xnh_b, Xh, Xh_b)
        load_and_norm1(0)
        late_setup()
        for ti, (base, N, full) in enumerate(tiles):
            first = ti == 0
            xT, xT_b = X[ti % 2], X_b[ti % 2]
            cvo, cvo_b = X[(ti + 1) % 2], X_b[(ti + 1) % 2]
            ffn_gu(0, N, halo=first)
            if first:
                sv = (cos_t, cos_b, sin_t, sin_b)
                cos_t, cos_b, sin_t, sin_b = cos_h, cosh_b, sin_h, sinh_b
                rope_tables(0, HALO)
                cos_t, cos_b, sin_t, sin_b = sv
            rope_tables(base, N)
            ffn_down(0, N, halo=first)
            if first:
                rmsnorm(HALO, C_MIX, xnh, xnh_b, Xh, Xh_b)
                rmsnorm(N, C_MIX, xn, xn_b)
                sv = (xT, xT_b, xn, xn_b, cos_t, cos_b, sin_t, sin_b)
                xT, xT_b, xn, xn_b = Xh, Xh_b, xnh, xnh_b
                cos_t, cos_b, sin_t, sin_b = cos_h, cosh_b, sin_h, sinh_b
                mixer(-1, 0, HALO, False, do_norm=False)
                xT, xT_b, xn, xn_b, cos_t, cos_b, sin_t, sin_b = sv
            mixer(ti, base, N, full, do_norm=not first)
            rmsnorm(N, C_FFN2, xn, xn_b)
            ffn_gu(1, N)
            if ti + 1 < len(tiles):
                load_and_norm1(ti + 1)
            ffn_down(1, N)
            pst, pb = stat_bank(1)
            for c in range(8):
                s_t, s_b = getsq()
                op(ACT, lambda c=c, s_t=s_t: nc.scalar.activation(out=s_t[:, 0:N], in_=xT[:, c, 0:N], func=AF.Square),
                   rd=(xT_b[c],), wr=(s_b,))
                op(PE, lambda c=c, s_t=s_t: nc.tensor.matmul(pst[:, 0:N], lhsT=ones[:], rhs=s_t[:, 0:N], start=(c == 0), stop=(c == 7)),
                   rd=(ones_b, s_b), wr=(pb,))
            t_t, t_b = gettmp()
            op(ACT, lambda: nc.scalar.activation(out=t_t[:, 0:N], in_=pst[:, 0:N], func=AF.Ln, scale=1.0 / D, bias=eps_col(EPS)),
               rd=(pb, misc_b), wr=(t_b,))
            op(ACT, lambda: nc.scalar.activation(out=pst[:, 0:N], in_=t_t[:, 0:N], func=AF.Exp, scale=-0.5),
               rd=(t_b,), wr=(pb,))
            o0 = base - HALO
            odst = outT_d[:, o0:o0 + N].rearrange("(c p) t -> p c t", p=128)
            for c in range(8):
                op(DVE, lambda c=c: nc.vector.scalar_tensor_tensor(
                    out=xT[:, c, 0:N], in0=xT[:, c, 0:N], scalar=cols[:, C_FIN, c:c + 1], in1=pst[:, 0:N],
                    op0=ALU.mult, op1=ALU.mult),
                   rd=(xT_b[c], cols_b, pb), wr=(xT_b[c],))
                if c == 3:
                    dma(SP, s_out, odst[:, 0:4, :], xT[:, 0:4, 0:N], rd=xT_b[0:4])
            dma(SP, s_out2, odst[:, 4:8, :], xT[:, 4:8, 0:N], rd=xT_b[4:8])
        SP.wait(s_out, s_out.n)
        SP.wait(s_out2, s_out2.n)
    return nc, (SA.rec, SD.rec)


def _pack_shared(inp):
    f = np.float32
    w_in = np.asarray(inp["w_in"][0], f)
    bnds = np.cumsum([1024, 1024, 1024, 256, 256, 1024, 1024])
    glu_a, glu_b = w_in[:, :bnds[0]], w_in[:, bnds[0]:bnds[1]]
    wq, wk, wv = w_in[:, bnds[1]:bnds[2]], w_in[:, bnds[2]:bnds[3]], w_in[:, bnds[3]:bnds[4]]
    g_conv, g_attn = w_in[:, bnds[4]:bnds[5]], w_in[:, bnds[5]:bnds[6]]

    def rot(wh):
        n = wh.shape[1] // 64
        return wh.reshape(D, n, 2, 32)[:, :, ::-1, :].reshape(D, n * 64)


    def qchunk(w, qc):
        gp, i = qc // 4, qc % 4
        h0, h1 = 4 * (2 * gp) + i, 4 * (2 * gp + 1) + i
        return np.concatenate([w[:, h0 * 64:(h0 + 1) * 64], w[:, h1 * 64:(h1 + 1) * 64]], axis=1)

    blocks = []

    def ffn_blocks(wg, wu):
        for b in range(11):
            blocks.append(wg[:, 256 * b:256 * b + 256])
            blocks.append(wu[:, 256 * b:256 * b + 256])

    ffn_blocks(np.asarray(inp["ffn1_w_gate"][0], f), np.asarray(inp["ffn1_w_up"][0], f))
    blocks.append(wk)
    blocks.append(wv)
    for j in range(4):
        blocks.append(np.concatenate([qchunk(wq, 2 * j), qchunk(wq, 2 * j + 1)], axis=1))
    for cc in range(8):
        blocks.append(np.concatenate([glu_a[:, 128 * cc:128 * cc + 128], glu_b[:, 128 * cc:128 * cc + 128]], axis=1))
    w_cp = np.asarray(inp["conv_w_proj"][0], f)
    w_o = np.asarray(inp["attn_w_o"][0], f)
    perm = np.zeros(D, np.int64)
    for hc in range(8):
        gp, i = hc // 4, hc % 4
        for e in range(2):
            h = 4 * (2 * gp + e) + i
            perm[hc * 128 + e * 64:hc * 128 + e * 64 + 64] = np.arange(h * 64, h * 64 + 64)
    w_o_p = w_o[perm, :]
    for dc in range(8):
        blocks.append(np.concatenate([g_conv[:, 128 * dc:128 * dc + 128], g_attn[:, 128 * dc:128 * dc + 128]], axis=1))
        blocks.append(np.concatenate([w_cp[:, 128 * dc:128 * dc + 128], w_o_p[:, 128 * dc:128 * dc + 128]], axis=1))
    w_out = np.asarray(inp["w_out"][0], f)
    for j in range(4):
        blocks.append(w_out[:, 256 * j:256 * j + 256])
    ffn_blocks(np.asarray(inp["ffn2_w_gate"][0], f), np.asarray(inp["ffn2_w_up"][0], f))
    assert len(blocks) == NBLK_A
    wall = np.ascontiguousarray(np.concatenate(blocks, axis=1))
    wd = np.ascontiguousarray(np.concatenate([np.asarray(inp["ffn1_w_down"][0], f), np.asarray(inp["ffn2_w_down"][0], f)], axis=0))

    gate_b = np.asarray(inp["gate_b"][0], f)
    vecs = [inp["ffn1_norm"][0], inp["mix_norm"][0], inp["ffn2_norm"][0], inp["final_norm"], inp["conv_dw_b"][0],
            inp["conv_ln_g"][0], inp["conv_ln_b"][0], gate_b[:1024], gate_b[1024:]]
    cols = np.stack([np.asarray(v, f).reshape(8, 128) for v in vecs], axis=0)
    cols = np.ascontiguousarray(cols.transpose(2, 0, 1))
    dw = np.zeros((32, D), f)
    dw[:CW] = np.asarray(inp["conv_dw_w"][0], f)
    dw = dw.reshape(16, 2, 8, 2, 64)
    w2 = np.zeros((8, 2, 64, 2, 16, 64), f)
    ii = np.arange(64)
    w2[:, :, ii, :, :, ii] = dw.transpose(4, 2, 1, 3, 0)
    w2 = np.ascontiguousarray(w2.reshape(8, 128, 2048))
    misc = np.zeros((128, 16), f)
    p = np.arange(128)
    inv_freq = (np.float32(10000.0) ** (-np.arange(32, dtype=f) / np.float32(32))).astype(f)
    misc[:, 0] = inv_freq[p % 32]
    misc[:, 1] = np.where((p % 64) < 32, -1.0, 1.0)
    sinks = np.asarray(inp["attn_sinks"][0], f)
    for gp in range(2):
        for i in range(4):
            for e in range(2):
                misc[64 * e:64 * e + 64, 2 + gp * 4 + i] = sinks[4 * (2 * gp + e) + i]
    misc[:, 10] = EPS
    misc[:, 11] = LN_EPS
    misc[:, 12] = np.pi / 2
    ident = np.eye(128, dtype=f)
    pswap = np.zeros((128, 128), f)
    mm = np.arange(128)
    pswap[np.where((mm % 64) < 32, mm + 32, mm - 32), mm] = 1.0
    return dict(wall=wall, wd=wd, cols=cols, w2=w2, misc=misc, ident=ident, pswap=pswap)


def _masks(first):
    s = np.arange(128)[:, None]
    q = np.arange(128)[None, :]
    mc = np.where(q >= s, 0.0, NEG).astype(np.float32)
    mp = np.where(s > q, 0.0, NEG).astype(np.float32)
    mp0 = np.full((128, 128), NEG, np.float32) if first else mp
    m = np.stack([np.tile(mp, (1, 4)), np.tile(mp0, (1, 4)), np.tile(mc, (1, 4))], axis=1)
    return np.ascontiguousarray(m)


_NC_CACHE = {}


def kernel(**inputs):
    x = np.asarray(inputs["x"], np.float32)
    positions = np.asarray(inputs["positions"], np.int32)
    shared = _pack_shared(inputs)
    in_maps = []
    for c in range(8):
        b, ch = c // 4, c % 4
        t0 = ch * TOK
        xs = np.zeros((TL, D), np.float32)
        ps_ = np.zeros((TL,), np.int32)
        if ch == 0:
            xs[HALO:] = x[b, 0:TOK]
            ps_[HALO:] = positions[b, 0:TOK]
        else:
            xs[:] = x[b, t0 - HALO:t0 + TOK]
            ps_[:] = positions[b, t0 - HALO:t0 + TOK]
        m = dict(shared)
        m["xT"] = np.ascontiguousarray(xs.T)
        m["pos"] = np.ascontiguousarray(np.broadcast_to(ps_[None, :], (128, TL)))
        m["masks"] = _masks(ch == 0)
        in_maps.append(m)
    nc = build_nc()
    res = run_bass_kernel_spmd(nc, in_maps, core_ids=list(range(8)))
    out = np.empty((BATCH, SEQ, D), np.float32)
    for c in range(8):
        b, ch = c // 4, c % 4
        out[b, ch * TOK:(ch + 1) * TOK, :] = np.asarray(res.results[c]["outT"]).T
    return out
```

```python
import numpy as np
import concourse.bass as bass
import concourse.mybir as mybir
from concourse.bass_utils import run_bass_kernel_spmd
from contextlib import ExitStack

F32, BF16, I32 = mybir.dt.float32, mybir.dt.bfloat16, mybir.dt.int32
AF = mybir.ActivationFunctionType
ALU = mybir.AluOpType

D = 1024; DFF = 2816; NFC = 22; SEQ = 8192; BATCH = 2
TOK = 2048; HALO = 128; TL = TOK + HALO; NT = 512
CW = 31; EPS = 1e-6; LN_EPS = 1e-5
NEG = -30000.0
PI = float(np.pi); TWO_PI = float(2 * np.pi)
NBLK_A = 78
SAME_ENGINE_SYNC = True
DEBUG = False

C_FFN1, C_MIX, C_FFN2, C_FIN, C_DWB, C_LNG, C_LNB, C_GBC, C_GBA = range(9)


class Sem:
    def __init__(self, h):
        self.h = h
        self.n = 0


class Eng:
    def __init__(self, e, sem):
        self.e = e
        self.sem = sem
        self.known = {}
        self.no_self = False

    def wait(self, sem, val):
        if val <= 0:
            return
        if sem is self.sem:
            if val > sem.n or not SAME_ENGINE_SYNC or self.no_self:
                return
        if self.known.get(sem, 0) >= val:
            return
        self.e.wait_ge(sem.h, val)
        self.known[sem] = val


class Buf:
    __slots__ = ("w", "r", "psum", "indep")

    def __init__(self, psum=False, indep=False):
        self.w = {}
        self.r = {}
        self.psum = psum
        self.indep = indep


def _deps(rd, wr, me=None):
    deps = {}
    for b in rd:
        for s, v in b.w.items():
            if deps.get(s, 0) < v:
                deps[s] = v
        if b.psum:
            for s, v in b.r.items():
                if s is not me and deps.get(s, 0) < v:
                    deps[s] = v
    for b in wr:
        for s, v in b.w.items():
            if (s is me and b.indep):
                continue
            if deps.get(s, 0) < v:
                deps[s] = v
        for s, v in b.r.items():
            if deps.get(s, 0) < v:
                deps[s] = v
    return deps


def op(eng, fn, rd=(), wr=()):
    for s, v in _deps(rd, wr, eng.sem).items():
        eng.wait(s, v)
    inst = fn()
    eng.sem.n += 1
    inst.then_inc(eng.sem.h, 1)
    ev = eng.sem.n
    for b in wr:
        b.w = {eng.sem: ev}
        b.r = {}
    for b in rd:
        if b.r.get(eng.sem, 0) < ev:
            b.r[eng.sem] = ev


def dma(q, dsem, out_ap, in_ap, rd=(), wr=()):
    for s, v in _deps(rd, wr, None).items():
        q.wait(s, v)
    q.wait(dsem, dsem.n)
    inst = q.e.dma_start(out=out_ap, in_=in_ap)
    dsem.n += 16
    inst.then_inc(dsem.h, 16)
    for b in wr:
        b.w = {dsem: dsem.n}
        b.r = {}
    for b in rd:
        b.r[dsem] = dsem.n


class Stream:
    def __init__(self, q, slots, bufs, sems, keys, src, live=2):
        self.live = live
        self.q, self.slots, self.bufs, self.sems, self.keys, self.src = q, slots, bufs, sems, keys, src
        self.depth = len(slots)
        self.i = 0
        self.issued = 0
        self.rec = []
        if keys is not None:
            for _ in range(self.depth - self.live):
                self._issue()

    def _issue(self):
        j = self.issued
        if j >= len(self.keys):
            return
        s = j % self.depth
        dma(self.q, self.sems[s], self.slots[s], self.src(self.keys[j]), wr=(self.bufs[s],))
        self.issued += 1

    def next(self, key):
        i = self.i
        self.i += 1
        s = i % self.depth
        if self.keys is None:
            self.rec.append(key)
        else:
            assert self.keys[i] == key, (self.keys[i], key)
            self._issue()
        return self.slots[s], self.bufs[s]


def build_nc():
    _, rec = _build(None)
    nc, _ = _build(rec)
    return nc


def _build(rec):
    nc = bass.Bass("TRN2", target_bir_lowering=False)
    dt = nc.dram_tensor
    xT_d = dt("xT", [D, TL], F32, kind="ExternalInput").ap()
    pos_d = dt("pos", [128, TL], I32, kind="ExternalInput").ap()
    masks_d = dt("masks", [128, 3, 512], F32, kind="ExternalInput").ap()
    wall_d = dt("wall", [D, NBLK_A * 256], F32, kind="ExternalInput").ap()
    wd_d = dt("wd", [2 * DFF, D], F32, kind="ExternalInput").ap()
    cols_d = dt("cols", [128, 9, 8], F32, kind="ExternalInput").ap()
    w2_d = dt("w2", [8, 128, 2048], F32, kind="ExternalInput").ap()
    misc_d = dt("misc", [128, 16], F32, kind="ExternalInput").ap()
    ident_d = dt("ident", [128, 128], F32, kind="ExternalInput").ap()
    pswap_d = dt("pswap", [128, 128], F32, kind="ExternalInput").ap()
    outT_d = dt("outT", [D, TOK], F32, kind="ExternalOutput").ap()
    dbg_d = None
    if DEBUG:
        dbg_d = dt("dbg", [6, D, NT], F32, kind="ExternalOutput").ap()

    with ExitStack() as es:
        def sb(name, shape, dtype):
            return es.enter_context(nc.sbuf_tensor(name, shape, dtype))

        def mksem(name):
            return Sem(es.enter_context(nc.semaphore(name)))

        PE = Eng(nc.tensor, mksem("s_pe"))
        PE.no_self = True
        ACT = Eng(nc.scalar, mksem("s_act"))
        DVE = Eng(nc.vector, mksem("s_dve"))
        POOL = Eng(nc.gpsimd, mksem("s_pool"))
        SP = Eng(nc.sync, mksem("s_sp"))

        X = [sb(f"X{i}", [128, 8, NT], F32) for i in range(2)]; X_b = [[Buf() for _ in range(8)] for _ in range(2)]
        xT, xT_b = X[0], X_b[0]
        cvo, cvo_b = X[1], X_b[1]
        xn = sb("xn", [128, 8, NT], BF16); xn_b = [Buf() for _ in range(8)]
        Hh = sb("Hh", [128, NFC, NT], BF16); H_b = [Buf() for _ in range(NFC)]
        NTMP = 5
        tmp = [sb(f"tmp{i}", [128, NT], F32) for i in range(NTMP)]; tmp_b = [Buf() for _ in range(NTMP)]
        NSQ = 4
        sq = [sb(f"sq{i}", [128, NT], BF16) for i in range(NSQ)]; sq_b = [Buf() for _ in range(NSQ)]
        NA = 7
        wa = [sb(f"wa{i}", [128, 8, 256], BF16) for i in range(NA)]; wa_b = [Buf() for _ in range(NA)]
        wa_s = [mksem(f"s_wa{i}") for i in range(NA)]
        ND = 7
        wdd = [sb(f"wd{i}", [128, 2, 512], BF16) for i in range(ND)]; wd_b = [Buf() for _ in range(ND)]
        wd_s = [mksem(f"s_wd{i}") for i in range(ND)]
        uT = sb("uT", [128, 8, 30 + NT], BF16); uT_b = [Buf() for _ in range(8)]
        RR = [[sb(f"RR{i}{g}", [128, 30 + NT], BF16) for g in range(2)] for i in range(2)]
        RR_b = [[Buf(indep=True) for g in range(2)] for i in range(2)]
        cT = sb("cT", [128, 8, NT], BF16); cT_b = [Buf() for _ in range(8)]
        qT = sb("qT", [128, 8, NT], BF16); qT_b = [Buf() for _ in range(8)]
        aT = sb("aT", [128, 8, NT], BF16); aT_b = [[Buf() for _ in range(4)] for _ in range(2)]
        mg, mg_b = qT, qT_b
        kTz = sb("kTz", [128, 4, 8 * 128], BF16); kTz_b = [[Buf() for _ in range(8)] for _ in range(4)]
        vz = sb("vz", [128, 8, 512], BF16); vz_b = [Buf() for _ in range(8)]
        cos_t = sb("cos_t", [128, NT], F32); cos_b = Buf()
        sin_t = sb("sin_t", [128, NT], F32); sin_b = Buf()
        posi = sb("posi", [128, NT], I32); posi_b = Buf()
        NPT = 6
        PT = [sb(f"PT{i}", [128, 512], BF16) for i in range(NPT)]; PT_b = [Buf() for _ in range(NPT)]
        masks = sb("masks_sb", [128, 3, 512], BF16); masks_b = Buf()
        sinkt = sb("sinkt", [128, 2, 512], F32); sinkt_b = Buf()
        ident = sb("ident_sb", [128, 128], BF16); ident_b = Buf()
        pswap = sb("pswap_sb", [128, 128], BF16); pswap_b = Buf()
        ones = sb("ones_sb", [128, 128], BF16); ones_b = Buf()
        onesz = sb("onesz", [128, 2, 128], BF16); onesz_b = Buf()
        cols = sb("cols_sb", [128, 9, 8], F32); cols_b = Buf()
        misc = sb("misc_sb", [128, 16], F32); misc_b = Buf()
        Xh = sb("Xh", [128, 8, HALO], F32); Xh_b = [Buf() for _ in range(8)]
        xnh = sb("xnh", [128, 8, HALO], BF16); xnh_b = [Buf() for _ in range(8)]
        Hhh = sb("Hhh", [128, NFC, HALO], BF16); Hhh_b = [Buf() for _ in range(NFC)]
        cos_h = sb("cos_h", [128, HALO], F32); cosh_b = Buf()
        sin_h = sb("sin_h", [128, HALO], F32); sinh_b = Buf()
        scr = sb("scr", [128, 2], F32); scr_b = Buf()
        s_x = mksem("s_x")
        s_pos = mksem("s_pos")
        s_out = mksem("s_out")
        s_out2 = mksem("s_out2")

        ps = [es.enter_context(nc.psum_tensor(f"ps{i}", [128, NT], F32)) for i in range(8)]
        ps_b = [Buf(psum=True) for _ in range(8)]
        ps_ctr = [0]

        def bank():
            i = ps_ctr[0] % 6
            ps_ctr[0] += 1
            return ps[i], ps_b[i]

        def stat_bank(i):
            return ps[6 + i], ps_b[6 + i]

        tmp_ctr = [0]

        def gettmp():
            i = tmp_ctr[0] % NTMP
            tmp_ctr[0] += 1
            return tmp[i], tmp_b[i]

        sq_ctr = [0]

        def getsq():
            i = sq_ctr[0] % NSQ
            sq_ctr[0] += 1
            return sq[i], sq_b[i]

        pt_ctr = [0]

        def getpt():
            i = pt_ctr[0] % NPT
            pt_ctr[0] += 1
            return PT[i], PT_b[i]

        def cdma(q, name, dst, src, buf):
            dma(q, mksem("s_c_" + name), dst, src, wr=(buf,))

        cdma(SP, "cols", cols[:], cols_d, cols_b)
        cdma(SP, "misc", misc[:], misc_d, misc_b)
        op(DVE, lambda: nc.vector.memset(ones[:], 1.0), wr=(ones_b,))

        def late_setup():
            cdma(POOL, "ident", ident[:], ident_d, ident_b)
            cdma(POOL, "pswap", pswap[:], pswap_d, pswap_b)
            cdma(POOL, "masks", masks[:], masks_d, masks_b)
            op(DVE, lambda: nc.vector.memset(onesz[:], 0.0), wr=(onesz_b,))
            op(DVE, lambda: nc.vector.memset(onesz[:, 0, 0:64], 1.0), wr=(onesz_b,))
            op(DVE, lambda: nc.vector.memset(onesz[:, 1, 64:128], 1.0), wr=(onesz_b,))
            allk = [b_ for g_ in kTz_b for b_ in g_]
            op(DVE, lambda: nc.vector.memset(kTz[:], 0.0), wr=allk)
            op(DVE, lambda: nc.vector.memset(vz[:], 0.0), wr=vz_b)
            op(DVE, lambda: nc.vector.memset(uT[:], 0.0), wr=uT_b)
            for i_ in range(2):
                for g_ in range(2):
                    op(DVE, lambda i_=i_, g_=g_: nc.vector.memset(RR[i_][g_][:], 0.0), wr=(RR_b[i_][g_],))
            for gp in range(2):
                for i in range(4):
                    op(ACT, lambda gp=gp, i=i: nc.scalar.activation(
                        out=sinkt[:, gp, i * 128:(i + 1) * 128], in_=ident[:], func=AF.Exp,
                        bias=misc[:, 2 + gp * 4 + i:3 + gp * 4 + i], scale=0.0),
                       rd=(ident_b, misc_b), wr=(sinkt_b,))

        def ablk(j):
            return wall_d[:, j * 256:(j + 1) * 256].rearrange("(kc p) j -> p kc j", p=128)

        def dblk(f, h, b):
            r0 = f * DFF + b * 256
            return wd_d[r0:r0 + 256, h * 512:(h + 1) * 512].rearrange("(j p) c -> p j c", p=128)

        def asrc(key):
            if key[0] == "W":
                return w2_d[key[1]].rearrange("p (a b) -> p a b", a=8)
            return ablk(key[1])

        def dsrc(key):
            return dblk(key[1], key[2], key[3])

        SA = Stream(POOL, [w[:] for w in wa], wa_b, wa_s, None if rec is None else rec[0], asrc)
        SD = Stream(POOL, [w[:] for w in wdd], wd_b, wd_s, None if rec is None else rec[1], dsrc, live=1)

        def proj(pst, pb, wslot, wbuf, sub, rhs_t, rhs_bufs, N):
            for kc in range(8):
                op(PE, lambda kc=kc: nc.tensor.matmul(
                    pst[:, 0:N], lhsT=wslot[:, kc, sub * 128:(sub + 1) * 128], rhs=rhs_t[:, kc, 0:N],
                    start=(kc == 0), stop=(kc == 7)),
                   rd=(wbuf, rhs_bufs[kc]), wr=(pb,))

        def rmsnorm(N, gcol, out_t, out_bufs, x_t=None, x_bufs=None):
            if x_t is None:
                x_t, x_bufs = xT, xT_b
            pst, pb = stat_bank(0)
            op(ACT, lambda: nc.scalar.activation(out=scr[:, 0:1], in_=misc[:, 10:11], func=AF.Ln), rd=(misc_b,), wr=(scr_b,))
            for c in range(8):
                s_t, s_b = getsq()
                op(ACT, lambda c=c, s_t=s_t: nc.scalar.activation(out=s_t[:, 0:N], in_=x_t[:, c, 0:N], func=AF.Square),
                   rd=(x_bufs[c],), wr=(s_b,))
                op(PE, lambda c=c, s_t=s_t: nc.tensor.matmul(pst[:, 0:N], lhsT=ones[:], rhs=s_t[:, 0:N],
                                                             start=(c == 0), stop=(c == 7)),
                   rd=(ones_b, s_b), wr=(pb,))
            t_t, t_b = gettmp()
            op(ACT, lambda: nc.scalar.activation(out=t_t[:, 0:N], in_=pst[:, 0:N], func=AF.Ln, scale=1.0 / D, bias=eps_col(EPS)),
               rd=(pb, misc_b), wr=(t_b,))
            op(ACT, lambda: nc.scalar.activation(out=pst[:, 0:N], in_=t_t[:, 0:N], func=AF.Exp, scale=-0.5),
               rd=(t_b,), wr=(pb,))
            for c in range(8):
                op(DVE, lambda c=c: nc.vector.scalar_tensor_tensor(
                    out=out_t[:, c, 0:N], in0=x_t[:, c, 0:N], scalar=cols[:, gcol, c:c + 1], in1=pst[:, 0:N],
                    op0=ALU.mult, op1=ALU.mult),
                   rd=(x_bufs[c], cols_b, pb), wr=(out_bufs[c],))

        def eps_col(v):
            return misc[:, 10:11] if v == EPS else misc[:, 11:12]

        def ffn_gu(f, N, halo=False):
            for b in range(11):
                wg, wgb = SA.next(("A", (56 if f else 0) + 2 * b))
                wu, wub = SA.next(("A", (56 if f else 0) + 2 * b + 1))
                for sub in range(2):
                    fc = 2 * b + sub
                    pg, pgb = bank()
                    pu, pub = bank()
                    proj(pg, pgb, wg, wgb, sub, xn, xn_b, N)
                    proj(pu, pub, wu, wub, sub, xn, xn_b, N)
                    t_t, t_b = gettmp()
                    op(ACT, lambda: nc.scalar.activation(out=t_t[:, 0:N], in_=pg[:, 0:N], func=AF.Silu),
                       rd=(pgb,), wr=(t_b,))
                    op(DVE, lambda: nc.vector.tensor_tensor(out=Hh[:, fc, 0:N], in0=pu[:, 0:N], in1=t_t[:, 0:N], op=ALU.mult),
                       rd=(pub, t_b), wr=(H_b[fc],))
                    if halo:
                        pgh, pghb = bank()
                        puh, puhb = bank()
                        proj(pgh, pghb, wg, wgb, sub, xnh, xnh_b, HALO)
                        proj(puh, puhb, wu, wub, sub, xnh, xnh_b, HALO)
                        th, thb = gettmp()
                        op(ACT, lambda: nc.scalar.activation(out=th[:, 0:HALO], in_=pgh[:, 0:HALO], func=AF.Silu),
                           rd=(pghb,), wr=(thb,))
                        op(DVE, lambda: nc.vector.tensor_tensor(out=Hhh[:, fc, :], in0=puh[:, 0:HALO], in1=th[:, 0:HALO], op=ALU.mult),
                           rd=(puhb, thb), wr=(Hhh_b[fc],))

        def ffn_down(f, N, halo=False):
            for h in range(2):
                banks = [bank() for _ in range(4)]
                if halo:
                    hb, hbb = bank()
                for b in range(11):
                    wdt, wdb = SD.next(("D", f, h, b))
                    for sub in range(2):
                        fc = 2 * b + sub
                        for dcl in range(4):
                            pst, pb = banks[dcl]
                            op(PE, lambda pst=pst, dcl=dcl: nc.tensor.matmul(
                                pst[:, 0:N], lhsT=wdt[:, sub, dcl * 128:(dcl + 1) * 128], rhs=Hh[:, fc, 0:N],
                                start=(fc == 0), stop=(fc == NFC - 1)),
                               rd=(wdb, H_b[fc]), wr=(pb,))
                        if halo:
                            for dcl in range(4):
                                op(PE, lambda dcl=dcl: nc.tensor.matmul(
                                    hb[:, dcl * HALO:(dcl + 1) * HALO], lhsT=wdt[:, sub, dcl * 128:(dcl + 1) * 128], rhs=Hhh[:, fc, :],
                                    start=(fc == 0 and dcl == 0), stop=(fc == NFC - 1 and dcl == 3), skip_group_check=True),
                                   rd=(wdb, Hhh_b[fc]), wr=(hbb,))
                for dcl in range(4):
                    dc = h * 4 + dcl
                    pst, pb = banks[dcl]
                    op(DVE, lambda pst=pst, dc=dc: nc.vector.scalar_tensor_tensor(
                        out=xT[:, dc, 0:N], in0=pst[:, 0:N], scalar=0.5, in1=xT[:, dc, 0:N],
                        op0=ALU.mult, op1=ALU.add),
                       rd=(pb, xT_b[dc]), wr=(xT_b[dc],))
                if halo:
                    for dcl in range(4):
                        dc = h * 4 + dcl
                        op(DVE, lambda dcl=dcl, dc=dc: nc.vector.scalar_tensor_tensor(
                            out=Xh[:, dc, :], in0=hb[:, dcl * HALO:(dcl + 1) * HALO], scalar=0.5, in1=Xh[:, dc, :],
                            op0=ALU.mult, op1=ALU.add),
                           rd=(hbb, Xh_b[dc]), wr=(Xh_b[dc],))

        def rope_tables(base, N):
            dma(SP, s_pos, posi[:, 0:N], pos_d[:, base:base + N], wr=(posi_b,))
            a_t, a_b = gettmp()
            r_t, r_b = gettmp()
            w_t, w_b = gettmp()
            op(DVE, lambda: nc.vector.tensor_copy(out=a_t[:, 0:N], in_=posi[:, 0:N]), rd=(posi_b,), wr=(a_b,))
            op(DVE, lambda: nc.vector.tensor_scalar(out=a_t[:, 0:N], in0=a_t[:, 0:N], scalar1=misc[:, 0:1], scalar2=None,
                                                    op0=ALU.mult), rd=(a_b, misc_b), wr=(a_b,))
            op(DVE, lambda: nc.vector.tensor_scalar(out=posi[:, 0:N], in0=a_t[:, 0:N], scalar1=1.0 / TWO_PI, scalar2=None,
                                                    op0=ALU.mult), rd=(a_b,), wr=(posi_b,))
            op(DVE, lambda: nc.vector.tensor_copy(out=r_t[:, 0:N], in_=posi[:, 0:N]), rd=(posi_b,), wr=(r_b,))
            op(DVE, lambda: nc.vector.scalar_tensor_tensor(out=r_t[:, 0:N], in0=r_t[:, 0:N], scalar=-TWO_PI, in1=a_t[:, 0:N],
                                                           op0=ALU.mult, op1=ALU.add), rd=(r_b, a_b), wr=(r_b,))
            op(DVE, lambda: nc.vector.tensor_scalar(out=w_t[:, 0:N], in0=r_t[:, 0:N], scalar1=PI, scalar2=-TWO_PI,
                                                    op0=ALU.is_gt, op1=ALU.mult), rd=(r_b,), wr=(w_b,))
            op(DVE, lambda: nc.vector.tensor_tensor(out=r_t[:, 0:N], in0=r_t[:, 0:N], in1=w_t[:, 0:N], op=ALU.add),
               rd=(r_b, w_b), wr=(r_b,))
            op(DVE, lambda: nc.vector.tensor_scalar(out=r_t[:, 0:N], in0=r_t[:, 0:N], scalar1=-PI, scalar2=PI,
                                                    op0=ALU.max, op1=ALU.min), rd=(r_b,), wr=(r_b,))
            op(ACT, lambda: nc.scalar.activation(out=sin_t[:, 0:N], in_=r_t[:, 0:N], func=AF.Sin, scale=misc[:, 1:2]),
               rd=(r_b, misc_b), wr=(sin_b,))
            op(DVE, lambda: nc.vector.scalar_tensor_tensor(out=w_t[:, 0:N], in0=r_t[:, 0:N], scalar=-1.0, in1=r_t[:, 0:N],
                                                           op0=ALU.mult, op1=ALU.max), rd=(r_b,), wr=(w_b,))
            op(ACT, lambda: nc.scalar.activation(out=cos_t[:, 0:N], in_=w_t[:, 0:N], func=AF.Sin, scale=-1.0, bias=misc[:, 12:13]),
               rd=(w_b, misc_b), wr=(cos_b,))

        def rope_pipeline(chunks, N, hook=None):
            LOOKR = 2
            queue = []

            def finish(item):
                pa_, pab_, qb_, qbb_, done_ = item
                pr, prb = bank()
                op(PE, lambda: nc.tensor.matmul(pr[:, 0:N], lhsT=pswap[:], rhs=qb_[:, 0:N], start=True, stop=True),
                   rd=(pswap_b, qbb_), wr=(prb,))
                t1, t1b = gettmp()
                t2, t2b = gettmp()
                op(DVE, lambda: nc.vector.tensor_tensor(out=t1[:, 0:N], in0=pa_[:, 0:N], in1=cos_t[:, 0:N], op=ALU.mult),
                   rd=(pab_, cos_b), wr=(t1b,))
                op(DVE, lambda: nc.vector.tensor_tensor(out=t2[:, 0:N], in0=pr[:, 0:N], in1=sin_t[:, 0:N], op=ALU.mult),
                   rd=(prb, sin_b), wr=(t2b,))
                done_(t1, t1b, t2, t2b)

            for ch in chunks:
                wslot, wbuf, sub, done = ch
                if callable(wslot):
                    wslot, wbuf = wslot()
                pa, pab = bank()
                proj(pa, pab, wslot, wbuf, sub, xn, xn_b, N)
                qb, qbb = getsq()
                op(ACT, lambda: nc.scalar.activation(out=qb[:, 0:N], in_=pa[:, 0:N], func=AF.Copy), rd=(pab,), wr=(qbb,))
                queue.append((pa, pab, qb, qbb, done))
                if hook is not None:
                    hook()
                    hook = None
                if len(queue) > LOOKR:
                    finish(queue.pop(0))
            while queue:
                finish(queue.pop(0))

        def mixer(tile_idx, base, N, full, do_norm=True):
            nblk = N // 128
            gb0 = base // 128
            if do_norm:
                rmsnorm(N, C_MIX, xn, xn_b)

            wk, wkb = SA.next(("A", 22))

            def k_done(gp):
                def done(t1, t1b, t2, t2b):
                    for e in range(2):
                        g = 2 * gp + e
                        for bl in range(nblk):
                            slot = (gb0 + bl) % 8
                            op(DVE, lambda: nc.vector.tensor_tensor(
                                out=kTz[64 * e:64 * e + 64, g, slot * 128:(slot + 1) * 128],
                                in0=t1[64 * e:64 * e + 64, bl * 128:(bl + 1) * 128],
                                in1=t2[64 * e:64 * e + 64, bl * 128:(bl + 1) * 128], op=ALU.add),
                               rd=(t1b, t2b), wr=(kTz_b[g][slot],))
                return done
            rope_pipeline([(wk, wkb, gp, k_done(gp)) for gp in range(2)], N)
            wv, wvb = SA.next(("A", 23))
            for bl in range(nblk):
                slot = (gb0 + bl) % 8
                pst, pb = bank()
                for kc in range(8):
                    op(PE, lambda kc=kc, bl=bl, pst=pst: nc.tensor.matmul(
                        pst[:, 0:256], lhsT=xn[:, kc, bl * 128:(bl + 1) * 128], rhs=wv[:, kc, 0:256],
                        start=(kc == 0), stop=(kc == 7)),
                       rd=(wvb, xn_b[kc]), wr=(pb,))
                for e in range(2):
                    op(ACT, lambda e=e, slot=slot, pst=pst: nc.scalar.activation(
                        out=vz[:, slot, :].rearrange("p (gp r) -> p gp r", gp=2)[:, :, e * 192:e * 192 + 64],
                        in_=pst[:, 0:256].rearrange("p (gp e d) -> p gp e d", gp=2, e=2)[:, :, e, :],
                        func=AF.Copy),
                       rd=(pb,), wr=(vz_b[slot],))
            s1p = s2p = None
            if full:
                s1p, s1b = stat_bank(0)
                s2p, s2b = stat_bank(1)
            pending = []

            def emit_glu(cc):
                wgl, wglb = SA.next(("A", 28 + cc))
                pa, pab = bank()
                pbk, pbb = bank()
                proj(pa, pab, wgl, wglb, 0, xn, xn_b, N)
                proj(pbk, pbb, wgl, wglb, 1, xn, xn_b, N)
                t_t, t_b = gettmp()
                op(ACT, lambda: nc.scalar.activation(out=t_t[:, 0:N], in_=pbk[:, 0:N], func=AF.Sigmoid),
                   rd=(pbb,), wr=(t_b,))
                op(DVE, lambda: nc.vector.tensor_tensor(out=uT[:, cc, 30:30 + N], in0=pa[:, 0:N], in1=t_t[:, 0:N], op=ALU.mult),
                   rd=(pab, t_b), wr=(uT_b[cc],))
                if full:
                    W_ = 30 + N
                    for g2 in range(2):
                        rr, rrb = RR[cc % 2][g2], RR_b[cc % 2][g2]
                        lo, hi = 64 * g2, 64 * g2 + 64
                        op(DVE, lambda: nc.vector.tensor_copy(out=rr[0:64, 0:W_], in_=uT[lo:hi, cc, 0:W_]),
                           rd=(uT_b[cc],), wr=(rrb,))
                        op(DVE, lambda: nc.vector.tensor_copy(out=rr[64:128, 0:W_ - 1], in_=uT[lo:hi, cc, 1:W_]),
                           rd=(uT_b[cc],), wr=(rrb,))

            def emit_conv(cc):
                nonlocal pending
                if full:
                    dg, dgb = SA.next(("W", cc))
                    pc, pcb = bank()
                    for j in range(16):
                        for g2 in range(2):
                            op(PE, lambda j=j, g2=g2: nc.tensor.matmul(
                                pc[64 * g2:64 * g2 + 64, 0:N], lhsT=dg[:, 4 * g2 + j // 4, (j % 4) * 64:(j % 4) * 64 + 64],
                                rhs=RR[cc % 2][g2][:, 2 * j:2 * j + N], start=(j == 0), stop=(j == 15)),
                               rd=(dgb, RR_b[cc % 2][g2]), wr=(pcb,))
                    for f_ in pending:
                        f_()
                    pending = []
                    op(DVE, lambda: nc.vector.tensor_scalar(out=cvo[:, cc, 0:N], in0=pc[:, 0:N], scalar1=cols[:, C_DWB, cc:cc + 1],
                                                            scalar2=None, op0=ALU.add),
                       rd=(pcb, cols_b), wr=(cvo_b[cc],))
                    a_t, a_b = getsq()
                    op(DVE, lambda: nc.vector.tensor_copy(out=a_t[:, 0:N], in_=cvo[:, cc, 0:N]),
                       rd=(cvo_b[cc],), wr=(a_b,))
                    q_t, q_b = getsq()
                    op(ACT, lambda: nc.scalar.activation(out=q_t[:, 0:N], in_=cvo[:, cc, 0:N], func=AF.Square),
                       rd=(cvo_b[cc],), wr=(q_b,))

                    def stats(cc=cc, a_t=a_t, a_b=a_b, q_t=q_t, q_b=q_b):
                        op(PE, lambda: nc.tensor.matmul(s1p[:, 0:N], lhsT=ones[:], rhs=a_t[:, 0:N], start=(cc == 0), stop=(cc == 7)),
                           rd=(ones_b, a_b), wr=(s1b,))
                        op(PE, lambda: nc.tensor.matmul(s2p[:, 0:N], lhsT=ones[:], rhs=q_t[:, 0:N], start=(cc == 0), stop=(cc == 7)),
                           rd=(ones_b, q_b), wr=(s2b,))
                    pending.append(stats)
                op(ACT, lambda: nc.scalar.activation(out=uT[:, cc, 0:30], in_=uT[:, cc, N:N + 30], func=AF.Copy),
                   rd=(uT_b[cc],), wr=(uT_b[cc],))

            emit_glu(0)
            for cc in range(8):
                if cc + 1 < 8:
                    emit_glu(cc + 1)
                emit_conv(cc)
            if not full:
                return

            def ln_prologue():
                m_t, m_b = gettmp()
                v_t, v_b = gettmp()
                r_t, r_b = gettmp()
                op(ACT, lambda: nc.scalar.activation(out=m_t[:, 0:N], in_=s1p[:, 0:N], func=AF.Square, scale=1.0 / D),
                   rd=(s1b,), wr=(m_b,))
                op(DVE, lambda: nc.vector.scalar_tensor_tensor(out=v_t[:, 0:N], in0=s2p[:, 0:N], scalar=1.0 / D, in1=m_t[:, 0:N],
                                                               op0=ALU.mult, op1=ALU.subtract), rd=(s2b, m_b), wr=(v_b,))
                op(ACT, lambda: nc.scalar.activation(out=v_t[:, 0:N], in_=v_t[:, 0:N], func=AF.Ln, bias=eps_col(LN_EPS)),
                   rd=(v_b, misc_b), wr=(v_b,))
                op(ACT, lambda: nc.scalar.activation(out=r_t[:, 0:N], in_=v_t[:, 0:N], func=AF.Exp, scale=-0.5),
                   rd=(v_b,), wr=(r_b,))
                op(DVE, lambda: nc.vector.scalar_tensor_tensor(out=m_t[:, 0:N], in0=s1p[:, 0:N], scalar=-1.0 / D, in1=r_t[:, 0:N],
                                                               op0=ALU.mult, op1=ALU.mult), rd=(s1b, r_b), wr=(m_b,))
                op(ACT, lambda: nc.scalar.activation(out=s2p[:, 0:N], in_=r_t[:, 0:N], func=AF.Copy),
                   rd=(r_b,), wr=(s2b,))
                op(ACT, lambda: nc.scalar.activation(out=s1p[:, 0:N], in_=m_t[:, 0:N], func=AF.Copy),
                   rd=(m_b,), wr=(s1b,))

            def ln_apply(cc):
                op(DVE, lambda: nc.vector.tensor_tensor(out=cvo[:, cc, 0:N], in0=cvo[:, cc, 0:N], in1=s2p[:, 0:N], op=ALU.mult),
                   rd=(cvo_b[cc], s2b), wr=(cvo_b[cc],))
                op(DVE, lambda: nc.vector.tensor_tensor(out=cvo[:, cc, 0:N], in0=cvo[:, cc, 0:N], in1=s1p[:, 0:N], op=ALU.add),
                   rd=(cvo_b[cc], s1b), wr=(cvo_b[cc],))
                op(ACT, lambda: nc.scalar.activation(out=cT[:, cc, 0:N], in_=cvo[:, cc, 0:N], func=AF.Silu,
                                                     scale=cols[:, C_LNG, cc:cc + 1], bias=cols[:, C_LNB, cc:cc + 1]),
                   rd=(cvo_b[cc], cols_b), wr=(cT_b[cc],))

            qw = {}

            def after_first_q():
                for f_ in pending:
                    f_()
                ln_prologue()

            def q_chunk(qc):
                def getw():
                    if qc % 2 == 0:
                        qw["w"] = SA.next(("A", 24 + qc // 2))
                    return qw["w"]

                def done(t1, t1b, t2, t2b):
                    op(POOL, lambda: nc.gpsimd.tensor_tensor(out=qT[:, qc, 0:N], in0=t1[:, 0:N], in1=t2[:, 0:N], op=ALU.add),
                       rd=(t1b, t2b), wr=(qT_b[qc],))
                    ln_apply(qc)
                return (getw, None, qc % 2, done)
            rope_pipeline([q_chunk(qc) for qc in range(8)], N, hook=after_first_q)
            pass
            steps = [(bl, gp, e) for bl in range(nblk) for gp in range(2) for e in range(2)]

            def emit_scores(step):
                bl, gp, e = step
                g = 2 * gp + e
                gblk = gb0 + bl
                res = []
                for kt, slot in ((0, (gblk - 1) % 8), (1, gblk % 8)):
                    pst, pb = bank()
                    op(PE, lambda: nc.tensor.matmul(
                        pst[:, :].rearrange("p (i q) -> p i q", i=4), lhsT=kTz[:, g, slot * 128:(slot + 1) * 128],
                        rhs=qT[:, gp * 4:(gp + 1) * 4, bl * 128:(bl + 1) * 128], start=True, stop=False),
                       rd=[kTz_b[g][slot]] + [qT_b[gp * 4 + i] for i in range(4)], wr=(pb,))
                    mi = 2 if kt == 1 else (1 if gblk == 1 else 0)
                    op(PE, lambda: nc.tensor.matmul(pst[:, :], lhsT=ident[:], rhs=masks[:, mi, :], start=False, stop=True),
                       rd=(ident_b, masks_b), wr=(pb,))
                    p_t, p_b = getpt()
                    op(ACT, lambda: nc.scalar.activation(out=p_t[:], in_=pst[:, :], func=AF.Exp, scale=0.125),
                       rd=(pb,), wr=(p_b,))
                    res.append((p_t, p_b, slot, g))
                return res

            LOOK = 2
            queue = [emit_scores(steps[i]) for i in range(min(LOOK, len(steps)))]
            pn, pnb = stat_bank(0)
            pd, pdb = stat_bank(1)
            for si, (bl, gp, e) in enumerate(steps):
                if si + LOOK < len(steps):
                    queue.append(emit_scores(steps[si + LOOK]))
                pend = queue.pop(0)
                for n_, (p_t, p_b, slot, g) in enumerate(pend):
                    op(PE, lambda: nc.tensor.matmul(pn[:, :], lhsT=vz[:, slot, g * 128:(g + 1) * 128], rhs=p_t[:],
                                                    start=(e == 0 and n_ == 0), stop=(e == 1 and n_ == 1)),
                       rd=(vz_b[slot], p_b), wr=(pnb,))
                for n_, (p_t, p_b, slot, g) in enumerate(pend):
                    op(PE, lambda: nc.tensor.matmul(pd[:, :], lhsT=onesz[:, e, :], rhs=p_t[:],
                                                    start=(e == 0 and n_ == 0), stop=(e == 1 and n_ == 1)),
                       rd=(onesz_b, p_b), wr=(pdb,))
                if e == 1:
                    d_t, d_b = gettmp()
                    n_t, n_b = gettmp()
                    op(DVE, lambda: nc.vector.tensor_tensor(out=d_t[:], in0=pd[:, :], in1=sinkt[:, gp, :], op=ALU.add),
                       rd=(pdb, sinkt_b), wr=(d_b,))
                    op(DVE, lambda: nc.vector.tensor_copy(out=n_t[:], in_=pn[:, :]), rd=(pnb,), wr=(n_b,))
                    op(ACT, lambda: nc.scalar.activation(out=d_t[:], in_=d_t[:], func=AF.Ln), rd=(d_b,), wr=(d_b,))
                    op(ACT, lambda: nc.scalar.activation(out=d_t[:], in_=d_t[:], func=AF.Exp, scale=-1.0), rd=(d_b,), wr=(d_b,))
                    op(DVE, lambda: nc.vector.tensor_tensor(
                        out=aT[:, gp * 4:(gp + 1) * 4, bl * 128:(bl + 1) * 128],
                        in0=n_t[:].rearrange("p (i q) -> p i q", i=4),
                        in1=d_t[:].rearrange("p (i q) -> p i q", i=4), op=ALU.mult),
                       rd=(n_b, d_b), wr=(aT_b[gp][bl],))
            aT_all = [aT_b[hc // 4] for hc in range(8)]
            for dc in range(8):
                wga, wgab = SA.next(("A", 36 + 2 * dc))
                wpb, wpbb = SA.next(("A", 37 + 2 * dc))
                pgc, pgcb = bank()
                pga, pgab = bank()
                pcp, pcpb = bank()
                pao, paob = bank()
                proj(pgc, pgcb, wga, wgab, 0, xn, xn_b, N)
                proj(pga, pgab, wga, wgab, 1, xn, xn_b, N)
                proj(pcp, pcpb, wpb, wpbb, 0, cT, cT_b, N)
                for hc in range(8):
                    op(PE, lambda hc=hc: nc.tensor.matmul(pao[:, 0:N], lhsT=wpb[:, hc, 128:256], rhs=aT[:, hc, 0:N],
                                                          start=(hc == 0), stop=(hc == 7)),
                       rd=[wpbb] + aT_all[hc], wr=(paob,))
                s1, s1bb = gettmp()
                s2, s2bb = gettmp()
                op(ACT, lambda: nc.scalar.activation(out=s1[:, 0:N], in_=pgc[:, 0:N], func=AF.Sigmoid, bias=cols[:, C_GBC, dc:dc + 1]),
                   rd=(pgcb, cols_b), wr=(s1bb,))
                op(ACT, lambda: nc.scalar.activation(out=s2[:, 0:N], in_=pga[:, 0:N], func=AF.Sigmoid, bias=cols[:, C_GBA, dc:dc + 1]),
                   rd=(pgab, cols_b), wr=(s2bb,))
                op(DVE, lambda: nc.vector.tensor_tensor(out=s1[:, 0:N], in0=pcp[:, 0:N], in1=s1[:, 0:N], op=ALU.mult),
                   rd=(pcpb, s1bb), wr=(s1bb,))
                op(DVE, lambda: nc.vector.tensor_tensor(out=s2[:, 0:N], in0=pao[:, 0:N], in1=s2[:, 0:N], op=ALU.mult),
                   rd=(paob, s2bb), wr=(s2bb,))
                op(DVE, lambda: nc.vector.tensor_tensor(out=mg[:, dc, 0:N], in0=s1[:, 0:N], in1=s2[:, 0:N], op=ALU.add),
                   rd=(s1bb, s2bb), wr=(mg_b[dc],))
            for j in range(4):
                wo, wob = SA.next(("A", 52 + j))
                for sub in range(2):
                    dc2 = 2 * j + sub
                    pst, pb = bank()
                    proj(pst, pb, wo, wob, sub, mg, mg_b, N)
                    op(DVE, lambda pst=pst, dc2=dc2: nc.vector.tensor_tensor(out=xT[:, dc2, 0:N], in0=pst[:, 0:N], in1=xT[:, dc2, 0:N], op=ALU.add),
                       rd=(pb, xT_b[dc2]), wr=(xT_b[dc2],))

        tiles = [(HALO + j * NT, NT, True) for j in range(4)]

        def load_and_norm1(ti):
            base, N, full = tiles[ti]
            x_t, x_bufs = X[ti % 2], X_b[ti % 2]
            dma(SP, s_x, x_t[:, :, 0:N], xT_d[:, base:base + N].rearrange("(c p) t -> p c t", p=128), wr=x_bufs)
            rmsnorm(N, C_FFN1, xn, xn_b, x_t, x_bufs)

        dma(SP, s_x, Xh[:], xT_d[:, 0:HALO].rearrange("(c p) t -> p c t", p=128), wr=Xh_b)
        rmsnorm(HALO, C_FFN1, xnh, xnh_b, Xh, Xh_b)
        load_and_norm1(0)
        late_setup()
        for ti, (base, N, full) in enumerate(tiles):
            first = ti == 0
            xT, xT_b = X[ti % 2], X_b[ti % 2]
            cvo, cvo_b = X[(ti + 1) % 2], X_b[(ti + 1) % 2]
            ffn_gu(0, N, halo=first)
            if first:
                sv = (cos_t, cos_b, sin_t, sin_b)
                cos_t, cos_b, sin_t, sin_b = cos_h, cosh_b, sin_h, sinh_b
                rope_tables(0, HALO)
                cos_t, cos_b, sin_t, sin_b = sv
            rope_tables(base, N)
            ffn_down(0, N, halo=first)
            if first:
                rmsnorm(HALO, C_MIX, xnh, xnh_b, Xh, Xh_b)
                rmsnorm(N, C_MIX, xn, xn_b)
                sv = (xT, xT_b, xn, xn_b, cos_t, cos_b, sin_t, sin_b)
                xT, xT_b, xn, xn_b = Xh, Xh_b, xnh, xnh_b
                cos_t, cos_b, sin_t, sin_b = cos_h, cosh_b, sin_h, sinh_b
                mixer(-1, 0, HALO, False, do_norm=False)
                xT, xT_b, xn, xn_b, cos_t, cos_b, sin_t, sin_b = sv
            mixer(ti, base, N, full, do_norm=not first)
            rmsnorm(N, C_FFN2, xn, xn_b)
            ffn_gu(1, N)
            if ti + 1 < len(tiles):
                load_and_norm1(ti + 1)
            ffn_down(1, N)
            pst, pb = stat_bank(1)
            for c in range(8):
                s_t, s_b = getsq()
                op(ACT, lambda c=c, s_t=s_t: nc.scalar.activation(out=s_t[:, 0:N], in_=xT[:, c, 0:N], func=AF.Square),
                   rd=(xT_b[c],), wr=(s_b,))
                op(PE, lambda c=c, s_t=s_t: nc.tensor.matmul(pst[:, 0:N], lhsT=ones[:], rhs=s_t[:, 0:N], start=(c == 0), stop=(c == 7)),
                   rd=(ones_b, s_b), wr=(pb,))
            t_t, t_b = gettmp()
            op(ACT, lambda: nc.scalar.activation(out=t_t[:, 0:N], in_=pst[:, 0:N], func=AF.Ln, scale=1.0 / D, bias=eps_col(EPS)),
               rd=(pb, misc_b), wr=(t_b,))
            op(ACT, lambda: nc.scalar.activation(out=pst[:, 0:N], in_=t_t[:, 0:N], func=AF.Exp, scale=-0.5),
               rd=(t_b,), wr=(pb,))
            o0 = base - HALO
            odst = outT_d[:, o0:o0 + N].rearrange("(c p) t -> p c t", p=128)
            for c in range(8):
                op(DVE, lambda c=c: nc.vector.scalar_tensor_tensor(
                    out=xT[:, c, 0:N], in0=xT[:, c, 0:N], scalar=cols[:, C_FIN, c:c + 1], in1=pst[:, 0:N],
                    op0=ALU.mult, op1=ALU.mult),
                   rd=(xT_b[c], cols_b, pb), wr=(xT_b[c],))
                if c == 3:
                    dma(SP, s_out, odst[:, 0:4, :], xT[:, 0:4, 0:N], rd=xT_b[0:4])
            dma(SP, s_out2, odst[:, 4:8, :], xT[:, 4:8, 0:N], rd=xT_b[4:8])
        SP.wait(s_out, s_out.n)
        SP.wait(s_out2, s_out2.n)
    return nc, (SA.rec, SD.rec)


def _pack_shared(inp):
    f = np.float32
    w_in = np.asarray(inp["w_in"][0], f)
    bnds = np.cumsum([1024, 1024, 1024, 256, 256, 1024, 1024])
    glu_a, glu_b = w_in[:, :bnds[0]], w_in[:, bnds[0]:bnds[1]]
    wq, wk, wv = w_in[:, bnds[1]:bnds[2]], w_in[:, bnds[2]:bnds[3]], w_in[:, bnds[3]:bnds[4]]
    g_conv, g_attn = w_in[:, bnds[4]:bnds[5]], w_in[:, bnds[5]:bnds[6]]

    def rot(wh):
        n = wh.shape[1] // 64
        return wh.reshape(D, n, 2, 32)[:, :, ::-1, :].reshape(D, n * 64)


    def qchunk(w, qc):
        gp, i = qc // 4, qc % 4
        h0, h1 = 4 * (2 * gp) + i, 4 * (2 * gp + 1) + i
        return np.concatenate([w[:, h0 * 64:(h0 + 1) * 64], w[:, h1 * 64:(h1 + 1) * 64]], axis=1)

    blocks = []

    def ffn_blocks(wg, wu):
        for b in range(11):
            blocks.append(wg[:, 256 * b:256 * b + 256])
            blocks.append(wu[:, 256 * b:256 * b + 256])

    ffn_blocks(np.asarray(inp["ffn1_w_gate"][0], f), np.asarray(inp["ffn1_w_up"][0], f))
    blocks.append(wk)
    blocks.append(wv)
    for j in range(4):
        blocks.append(np.concatenate([qchunk(wq, 2 * j), qchunk(wq, 2 * j + 1)], axis=1))
    for cc in range(8):
        blocks.append(np.concatenate([glu_a[:, 128 * cc:128 * cc + 128], glu_b[:, 128 * cc:128 * cc + 128]], axis=1))
    w_cp = np.asarray(inp["conv_w_proj"][0], f)
    w_o = np.asarray(inp["attn_w_o"][0], f)
    perm = np.zeros(D, np.int64)
    for hc in range(8):
        gp, i = hc // 4, hc % 4
        for e in range(2):
            h = 4 * (2 * gp + e) + i
            perm[hc * 128 + e * 64:hc * 128 + e * 64 + 64] = np.arange(h * 64, h * 64 + 64)
    w_o_p = w_o[perm, :]
    for dc in range(8):
        blocks.append(np.concatenate([g_conv[:, 128 * dc:128 * dc + 128], g_attn[:, 128 * dc:128 * dc + 128]], axis=1))
        blocks.append(np.concatenate([w_cp[:, 128 * dc:128 * dc + 128], w_o_p[:, 128 * dc:128 * dc + 128]], axis=1))
    w_out = np.asarray(inp["w_out"][0], f)
    for j in range(4):
        blocks.append(w_out[:, 256 * j:256 * j + 256])
    ffn_blocks(np.asarray(inp["ffn2_w_gate"][0], f), np.asarray(inp["ffn2_w_up"][0], f))
    assert len(blocks) == NBLK_A
    wall = np.ascontiguousarray(np.concatenate(blocks, axis=1))
    wd = np.ascontiguousarray(np.concatenate([np.asarray(inp["ffn1_w_down"][0], f), np.asarray(inp["ffn2_w_down"][0], f)], axis=0))

    gate_b = np.asarray(inp["gate_b"][0], f)
    vecs = [inp["ffn1_norm"][0], inp["mix_norm"][0], inp["ffn2_norm"][0], inp["final_norm"], inp["conv_dw_b"][0],
            inp["conv_ln_g"][0], inp["conv_ln_b"][0], gate_b[:1024], gate_b[1024:]]
    cols = np.stack([np.asarray(v, f).reshape(8, 128) for v in vecs], axis=0)
    cols = np.ascontiguousarray(cols.transpose(2, 0, 1))
    dw = np.zeros((32, D), f)
    dw[:CW] = np.asarray(inp["conv_dw_w"][0], f)
    dw = dw.reshape(16, 2, 8, 2, 64)
    w2 = np.zeros((8, 2, 64, 2, 16, 64), f)
    ii = np.arange(64)
    w2[:, :, ii, :, :, ii] = dw.transpose(4, 2, 1, 3, 0)
    w2 = np.ascontiguousarray(w2.reshape(8, 128, 2048))
    misc = np.zeros((128, 16), f)
    p = np.arange(128)
    inv_freq = (np.float32(10000.0) ** (-np.arange(32, dtype=f) / np.float32(32))).astype(f)
    misc[:, 0] = inv_freq[p % 32]
    misc[:, 1] = np.where((p % 64) < 32, -1.0, 1.0)
    sinks = np.asarray(inp["attn_sinks"][0], f)
    for gp in range(2):
        for i in range(4):
            for e in range(2):
                misc[64 * e:64 * e + 64, 2 + gp * 4 + i] = sinks[4 * (2 * gp + e) + i]
    misc[:, 10] = EPS
    misc[:, 11] = LN_EPS
    misc[:, 12] = np.pi / 2
    ident = np.eye(128, dtype=f)
    pswap = np.zeros((128, 128), f)
    mm = np.arange(128)
    pswap[np.where((mm % 64) < 32, mm + 32, mm - 32), mm] = 1.0
    return dict(wall=wall, wd=wd, cols=cols, w2=w2, misc=misc, ident=ident, pswap=pswap)


def _masks(first):
    s = np.arange(128)[:, None]
    q = np.arange(128)[None, :]
    mc = np.where(q >= s, 0.0, NEG).astype(np.float32)
    mp = np.where(s > q, 0.0, NEG).astype(np.float32)
    mp0 = np.full((128, 128), NEG, np.float32) if first else mp
    m = np.stack([np.tile(mp, (1, 4)), np.tile(mp0, (1, 4)), np.tile(mc, (1, 4))], axis=1)
    return np.ascontiguousarray(m)


_NC_CACHE = {}


def kernel(**inputs):
    x = np.asarray(inputs["x"], np.float32)
    positions = np.asarray(inputs["positions"], np.int32)
    shared = _pack_shared(inputs)
    in_maps = []
    for c in range(8):
        b, ch = c // 4, c % 4
        t0 = ch * TOK
        xs = np.zeros((TL, D), np.float32)
        ps_ = np.zeros((TL,), np.int32)
        if ch == 0:
            xs[HALO:] = x[b, 0:TOK]
            ps_[HALO:] = positions[b, 0:TOK]
        else:
            xs[:] = x[b, t0 - HALO:t0 + TOK]
            ps_[:] = positions[b, t0 - HALO:t0 + TOK]
        m = dict(shared)
        m["xT"] = np.ascontiguousarray(xs.T)
        m["pos"] = np.ascontiguousarray(np.broadcast_to(ps_[None, :], (128, TL)))
        m["masks"] = _masks(ch == 0)
        in_maps.append(m)
    nc = build_nc()
    res = run_bass_kernel_spmd(nc, in_maps, core_ids=list(range(8)))
    out = np.empty((BATCH, SEQ, D), np.float32)
    for c in range(8):
        b, ch = c // 4, c % 4
        out[b, ch * TOK:(ch + 1) * TOK, :] = np.asarray(res.results[c]["outT"]).T
    return out
```
